# Optimizing a Trainium2 kernel written in Bass

```python
import jax, jax.numpy as jnp
from jax import lax
import numpy as np

D_MODEL = 1024
BATCH = 4
SEQ = 4096
DEPTH = 1
DEC_BATCH = 128
DEC_SEQ = 8
PAST_LEN = 8192
PAGE_SIZE = 128

N_HEADS_A = 4
HEAD_DIM_A = 128
WIDTH_A = N_HEADS_A * HEAD_DIM_A
N_HEADS_B = 8
HEAD_DIM_B = 64
WIDTH_B = N_HEADS_B * HEAD_DIM_B
MIX_WIDTH = WIDTH_A + WIDTH_B
CONV_WIDTH = 4
CONV_CH = 3 * WIDTH_A
DELTA_CHUNK = 64
DILATED_PATTERNS = ((128, 1), (512, 4), (2048, 16))
MAX_WINDOW = 2048
Q_BLOCK = 128
D_FF = 4 * D_MODEL
NORM_EPS = 1e-6
SPLIT_SIZES = (3 * WIDTH_A, WIDTH_A, N_HEADS_A, N_HEADS_A, WIDTH_B, WIDTH_B, WIDTH_B)
PROJ_COLS = sum(SPLIT_SIZES)

kernel_name = "hymba_gdn_dilated_swa_step"


def rmsnorm(x, g):
    xf = x.astype(jnp.float32)
    xf = xf * lax.rsqrt(jnp.mean(xf * xf, axis=-1, keepdims=True) + NORM_EPS)
    return (xf * g.astype(jnp.float32)).astype(x.dtype)


def l2norm(x):
    xf = x.astype(jnp.float32)
    return xf * lax.rsqrt(jnp.sum(xf * xf, axis=-1, keepdims=True) + NORM_EPS)


def short_conv(u, buf, w):
    t = u.shape[1]
    up = jnp.concatenate([buf.astype(u.dtype), u], axis=1)
    y = up[:, 0:t] * w[0]
    for i in range(1, CONV_WIDTH):
        y = y + up[:, i:i + t] * w[i]
    return jax.nn.silu(y), up[:, -(CONV_WIDTH - 1):]


def gated_delta_chunked(q, k, v, beta, g, s0):
    b, t, h, _ = q.shape
    dv = v.shape[-1]
    c = DELTA_CHUNK
    n = -(-t // c)
    pad = n * c - t

    def to_chunks(a):
        a = jnp.pad(a.astype(jnp.float32), [(0, 0), (0, pad)] + [(0, 0)] * (a.ndim - 2))
        a = jnp.moveaxis(a, 2, 1)
        return a.reshape(b, h, n, c, *a.shape[3:])

    qc, kc, vc, bc, gc = (to_chunks(a) for a in (q, k, v, beta, g))
    G = jnp.cumsum(gc, axis=-1)
    causal = jnp.tril(jnp.ones((c, c), bool))
    strict = jnp.tril(jnp.ones((c, c), bool), -1)
    diff = G[..., :, None] - G[..., None, :]
    decay = jnp.where(causal, jnp.exp(jnp.where(causal, diff, 0.0)), 0.0)
    kb = kc * bc[..., None]
    vb = vc * bc[..., None]
    lmat = jnp.where(strict, jnp.einsum('bhnik,bhnjk->bhnij', kb, kc) * decay, 0.0)
    a_mat = lmat + jnp.eye(c, dtype=jnp.float32)
    u = lax.linalg.triangular_solve(a_mat, vb, left_side=True, lower=True, unit_diagonal=True)
    w = lax.linalg.triangular_solve(a_mat, kb * jnp.exp(G)[..., None], left_side=True, lower=True,
                                    unit_diagonal=True)
    qk = jnp.einsum('bhnik,bhnjk->bhnij', qc, kc) * decay
    q_dec = qc * jnp.exp(G)[..., None]
    g_last = G[..., -1]
    k_dec = kc * jnp.exp(g_last[..., None] - G)[..., None]

    def step(s, xs):
        u_n, w_n, qk_n, qd_n, kd_n, gl_n = xs
        v_new = u_n - jnp.einsum('bhck,bhkv->bhcv', w_n, s)
        o_n = jnp.einsum('bhck,bhkv->bhcv', qd_n, s) + jnp.einsum('bhij,bhjv->bhiv', qk_n, v_new)
        s = s * jnp.exp(gl_n)[..., None, None] + jnp.einsum('bhck,bhcv->bhkv', kd_n, v_new)
        return s, o_n

    xs = tuple(jnp.moveaxis(a, 2, 0) for a in (u, w, qk, q_dec, k_dec, g_last))
    s_final, o = lax.scan(step, s0.astype(jnp.float32), xs)
    o = jnp.moveaxis(o, 0, 2).reshape(b, h, n * c, dv)[:, :, :t]
    return jnp.moveaxis(o, 1, 2), s_final


def dilated_mixture(q, k_all, v_all, q_index):
    scale = HEAD_DIM_B ** -0.5
    qf = q.astype(jnp.float32)
    outs, lses = [], []
    for window, dil in DILATED_PATTERNS:
        offs = jnp.arange(window // dil + 1) * dil
        idx = q_index[:, None] - offs[None, :]
        valid = idx >= 0
        idx = jnp.maximum(idx, 0)
        kg = jnp.take(k_all, idx, axis=1).astype(jnp.float32)
        vg = jnp.take(v_all, idx, axis=1).astype(jnp.float32)
        s = jnp.einsum('bqhd,bqjhd->bqhj', qf, kg) * scale
        s = jnp.where(valid[None, :, None, :], s, -jnp.inf)
        lse = jax.nn.logsumexp(s, axis=-1)
        p = jnp.exp(s - lse[..., None])
        outs.append(jnp.einsum('bqhj,bqjhd->bqhd', p, vg))
        lses.append(lse)
    wts = jax.nn.softmax(jnp.stack(lses), axis=0)
    return jnp.einsum('gbqh,gbqhd->bqhd', wts, jnp.stack(outs))


def dilated_attention(q, k_all, v_all, base):
    b, t, h, dh = q.shape
    if t % Q_BLOCK == 0:
        nb = t // Q_BLOCK
        qb = q.reshape(b, nb, Q_BLOCK, h, dh).swapaxes(0, 1)

        def blk(args):
            i, qblk = args
            return dilated_mixture(qblk, k_all, v_all, base + i * Q_BLOCK + jnp.arange(Q_BLOCK))

        o = lax.map(blk, (jnp.arange(nb), qb))
        return o.swapaxes(0, 1).reshape(b, t, h, dh)
    return dilated_mixture(q, k_all, v_all, base + jnp.arange(t))


def trunk_layer(x, conv_buf, s0, k_past, v_past, norm1_g, w_in, conv_w, a_log, dt_bias,
                delta_norm_g, q_norm_g, k_norm_g, w_o, norm2_g, w_up, w_down):
    b, t, _ = x.shape
    hn = rmsnorm(x, norm1_g)
    proj = hn @ w_in
    cuts = [int(c) for c in np.cumsum(SPLIT_SIZES)[:-1]]
    qkv_a, z_a, b_a, a_a, q_b, k_b, v_b = jnp.split(proj, cuts, axis=-1)

    qkv_a, conv_new = short_conv(qkv_a, conv_buf, conv_w)
    qa, ka, va = jnp.split(qkv_a, 3, axis=-1)
    qa = l2norm(qa.reshape(b, t, N_HEADS_A, HEAD_DIM_A)) * (HEAD_DIM_A ** -0.5)
    ka = l2norm(ka.reshape(b, t, N_HEADS_A, HEAD_DIM_A))
    va = va.reshape(b, t, N_HEADS_A, HEAD_DIM_A)
    beta = jax.nn.sigmoid(b_a.astype(jnp.float32))
    g = -jnp.exp(a_log.astype(jnp.float32)) * jax.nn.softplus(
        a_a.astype(jnp.float32) + dt_bias.astype(jnp.float32))
    oa, s_new = gated_delta_chunked(qa, ka, va, beta, g, s0)
    oa = rmsnorm(oa, delta_norm_g) * jax.nn.silu(
        z_a.reshape(b, t, N_HEADS_A, HEAD_DIM_A).astype(jnp.float32))
    oa = oa.reshape(b, t, WIDTH_A).astype(x.dtype)

    qb = rmsnorm(q_b.reshape(b, t, N_HEADS_B, HEAD_DIM_B), q_norm_g)
    kb = rmsnorm(k_b.reshape(b, t, N_HEADS_B, HEAD_DIM_B), k_norm_g)
    vb = v_b.reshape(b, t, N_HEADS_B, HEAD_DIM_B)
    k_all = jnp.concatenate([k_past.astype(kb.dtype), kb], axis=1)
    v_all = jnp.concatenate([v_past.astype(vb.dtype), vb], axis=1)
    ob = dilated_attention(qb, k_all, v_all, k_past.shape[1])
    ob = ob.reshape(b, t, WIDTH_B).astype(x.dtype)

    h1 = x + jnp.concatenate([oa, ob], axis=-1) @ w_o
    hid = jnp.square(jax.nn.relu(rmsnorm(h1, norm2_g) @ w_up))
    y = h1 + hid @ w_down
    return y, kb, vb, s_new, conv_new


def setup_inputs(seed: int = 0) -> dict:
    key = jax.random.key(seed)
    ks = jax.random.split(key, 20)
    wbuf = min(MAX_WINDOW, PAST_LEN)
    f32 = jnp.float32
    dt = jnp.exp(jax.random.uniform(ks[10], (DEPTH, N_HEADS_A), f32, np.log(1e-3), np.log(1e-1)))
    return {
        "x_prompt": jax.random.normal(ks[0], (BATCH, SEQ, D_MODEL), f32),
        "x_sample": jax.random.normal(ks[1], (DEC_BATCH, DEC_SEQ, D_MODEL), f32),
        "cache_swa_k": jax.random.normal(ks[2], (DEPTH, DEC_BATCH, wbuf, N_HEADS_B, HEAD_DIM_B), f32),
        "cache_swa_v": jax.random.normal(ks[3], (DEPTH, DEC_BATCH, wbuf, N_HEADS_B, HEAD_DIM_B), f32),
        "state_delta": 0.1 * jax.random.normal(ks[4], (DEPTH, DEC_BATCH, N_HEADS_A, HEAD_DIM_A, HEAD_DIM_A), f32),
        "state_conv": jax.random.normal(ks[5], (DEPTH, DEC_BATCH, CONV_WIDTH - 1, CONV_CH), f32),
        "norm1_g": 1.0 + 0.05 * jax.random.normal(ks[6], (DEPTH, D_MODEL), f32),
        "w_in": jax.random.normal(ks[7], (DEPTH, D_MODEL, PROJ_COLS), f32) * D_MODEL ** -0.5,
        "conv_w": jax.random.normal(ks[8], (DEPTH, CONV_WIDTH, CONV_CH), f32) * CONV_WIDTH ** -0.5,
        "a_log": jnp.log(jax.random.uniform(ks[9], (DEPTH, N_HEADS_A), f32, 1.0, 16.0)),
        "dt_bias": dt + jnp.log(-jnp.expm1(-dt)),
        "delta_norm_g": 1.0 + 0.05 * jax.random.normal(ks[11], (DEPTH, HEAD_DIM_A), f32),
        "q_norm_g": 1.0 + 0.05 * jax.random.normal(ks[12], (DEPTH, HEAD_DIM_B), f32),
        "k_norm_g": 1.0 + 0.05 * jax.random.normal(ks[13], (DEPTH, HEAD_DIM_B), f32),
        "w_o": jax.random.normal(ks[14], (DEPTH, MIX_WIDTH, D_MODEL), f32) * MIX_WIDTH ** -0.5,
        "norm2_g": 1.0 + 0.05 * jax.random.normal(ks[15], (DEPTH, D_MODEL), f32),
        "w_up": jax.random.normal(ks[16], (DEPTH, D_MODEL, D_FF), f32) * D_MODEL ** -0.5,
        "w_down": jax.random.normal(ks[17], (DEPTH, D_FF, D_MODEL), f32) * D_FF ** -0.5,
    }


def reference(x_prompt, x_sample, cache_swa_k, cache_swa_v, state_delta, state_conv, norm1_g, w_in,
              conv_w, a_log, dt_bias, delta_norm_g, q_norm_g, k_norm_g, w_o, norm2_g, w_up, w_down):
    b, s, _ = x_prompt.shape
    pbuf = min(MAX_WINDOW, s)
    yp, ys = x_prompt, x_sample
    kp_l, vp_l, dp_l, cp_l, ks_l, vs_l, ds_l, cs_l = [], [], [], [], [], [], [], []
    for layer in range(DEPTH):
        params = (norm1_g[layer], w_in[layer], conv_w[layer], a_log[layer], dt_bias[layer],
                  delta_norm_g[layer], q_norm_g[layer], k_norm_g[layer], w_o[layer], norm2_g[layer],
                  w_up[layer], w_down[layer])
        zero_kv = jnp.zeros((b, 0, N_HEADS_B, HEAD_DIM_B), yp.dtype)
        yp, kp, vp, dp, cp = trunk_layer(
            yp, jnp.zeros((b, CONV_WIDTH - 1, CONV_CH), yp.dtype),
            jnp.zeros((b, N_HEADS_A, HEAD_DIM_A, HEAD_DIM_A), jnp.float32),
            zero_kv, zero_kv, *params)
        kp_l.append(kp[:, -pbuf:]); vp_l.append(vp[:, -pbuf:]); dp_l.append(dp); cp_l.append(cp)
        ys, kn, vn, dn, cn = trunk_layer(
            ys, state_conv[layer], state_delta[layer], cache_swa_k[layer], cache_swa_v[layer], *params)
        ks_l.append(kn); vs_l.append(vn); ds_l.append(dn); cs_l.append(cn)
    return (yp, ys, jnp.stack(kp_l), jnp.stack(vp_l), jnp.stack(dp_l), jnp.stack(cp_l),
            jnp.stack(ks_l), jnp.stack(vs_l), jnp.stack(ds_l), jnp.stack(cs_l))
```

```python
import contextlib
import numpy as np
import ml_dtypes
import concourse.bass as bass
import concourse.mybir as mybir
from concourse.bass_utils import run_bass_kernel_spmd

F32 = mybir.dt.float32
BF16 = mybir.dt.bfloat16
AF = mybir.ActivationFunctionType
ALU = mybir.AluOpType
AX = mybir.AxisListType

NCORES = 8
D = 1024
NPRE = 2048
NOWN = 2048
NSMP = 128
NLOC = NPRE + NOWN + NSMP
NOUT = NOWN + NSMP
EPS = 1e-6


class Sched:
    def __init__(self, nc, stack, n_dma_sems=48):
        self.nc = nc
        self.engs = {"pe": nc.tensor, "act": nc.scalar, "dve": nc.vector,
                     "pool": nc.gpsimd, "sp": nc.sync}
        self.sem = {k: stack.enter_context(nc.semaphore("s_" + k)) for k in self.engs}
        self.cnt = {k: 0 for k in self.engs}
        self.known = {k: {} for k in self.engs}
        self.snap = {}
        self.last_w = {}
        self.reads = {}
        self.dsem = [stack.enter_context(nc.semaphore("d%d" % i)) for i in range(n_dma_sems)]
        self.dcnt = [0] * n_dma_sems
        self.drr = 0
        self.ninst = 0

    def _semof(self, key):
        if isinstance(key, tuple):
            return self.dsem[key[1]]
        return self.sem[key]

    def _wait(self, e, deps):
        kn = self.known[e]
        for (k, v) in sorted(deps, key=lambda t: str(t)):
            if kn.get(k, 0) >= v:
                continue
            if k == "pe" and e == "pe":
                continue
            self.engs[e].wait_ge(self._semof(k), v)
            sn = self.snap.get((k, v))
            if sn:
                for k2, v2 in sn.items():
                    if kn.get(k2, 0) < v2:
                        kn[k2] = v2
            kn[k] = v

    @staticmethod
    def _expand(keys):
        out = []
        for k in keys:
            if isinstance(k, tuple) and len(k) == 2 and k[0] == "bank":
                out.extend(("b", k[1], q) for q in range(4))
            else:
                out.append(k)
        return out

    def _deps(self, reads, writes):
        deps = set()
        for r in reads:
            lw = self.last_w.get(r)
            if lw:
                deps.add(lw)
        for w in writes:
            lw = self.last_w.get(w)
            if lw:
                deps.add(lw)
            for rd in self.reads.get(w, ()):
                deps.add(rd)
        return deps

    def _record(self, src, reads, writes):
        for r in reads:
            self.reads.setdefault(r, []).append(src)
        for w in writes:
            self.last_w[w] = src
            self.reads[w] = []

    @staticmethod
    def _bankx(reads, writes):
        banks = {k[1] for k in list(reads) + list(writes) if isinstance(k, tuple) and len(k) == 3 and k[0] == "b"}
        return list(writes) + [("bx", i) for i in sorted(banks)]

    def op(self, e, fn, reads=(), writes=()):
        reads, writes = self._expand(reads), self._expand(writes)
        writes = self._bankx(reads, writes)
        self._wait(e, self._deps(reads, writes))
        inst = fn(self.engs[e])
        self.cnt[e] += 1
        n = self.cnt[e]
        inst.then_inc(self.sem[e], 1)
        self.ninst += 1
        sn = dict(self.known[e])
        sn[e] = n
        self.snap[(e, n)] = sn
        self._record((e, n), reads, writes)
        return inst

    def prewait(self, e, rw_list):
        deps = set()
        for reads, writes in rw_list:
            reads, writes = self._expand(reads), self._expand(writes)
            writes = self._bankx(reads, writes)
            deps |= self._deps(reads, writes)
        self._wait(e, deps)

    def dma(self, q, out, in_, reads=(), writes=(), **kw):
        i = self.drr
        self.drr = (self.drr + 1) % len(self.dsem)
        reads, writes = self._expand(reads), self._expand(writes)
        deps = self._deps(reads, writes)
        if self.dcnt[i]:
            deps.add((("d", i), self.dcnt[i]))
        self._wait(q, deps)
        inst = self.engs[q].dma_start(out=out, in_=in_, **kw)
        self.dcnt[i] += 16
        inst.then_inc(self.dsem[i], 16)
        src = (("d", i), self.dcnt[i])
        self.snap[src] = dict(self.known[q])
        self._record(src, reads, writes)
        self.ninst += 1
        return inst

    def barrier(self):
        deps = set()
        for k in self.engs:
            if self.cnt[k]:
                deps.add((k, self.cnt[k]))
        for i, c in enumerate(self.dcnt):
            if c:
                deps.add((("d", i), c))
        for e in self.engs:
            self._wait(e, deps)

    def finish(self):
        deps = set()
        for k in self.engs:
            if self.cnt[k]:
                deps.add((k, self.cnt[k]))
        for i, c in enumerate(self.dcnt):
            if c:
                deps.add((("d", i), c))
        self._wait("sp", deps)


def mult_m(delta):
    d = np.asarray(delta)
    m = ((d >= 0) & (d <= 128)).astype(np.float32)
    m += ((d >= 0) & (d <= 512) & (d % 4 == 0))
    m += ((d >= 0) & (d <= 2048) & (d % 16 == 0))
    return m.astype(np.float32)


NEG = -30000.0


def host_consts():
    c = {}
    c["ident"] = np.eye(128, dtype=np.float32)
    j = np.arange(128)[:, None]
    i = np.arange(128)[None, :]
    same = (j // 8) == (i // 8)
    cm = np.zeros((128, 11, 128), np.float32)
    cm[:, 0] = (j <= i)
    cm[:, 1] = (j <= i) & same
    cm[:, 2] = same
    cm[:, 3] = np.where(i >= j, 0.0, NEG)
    cm[:, 4] = np.where(i > j, 0.0, NEG)
    cm[:, 5] = np.where(i < j, 0.0, NEG)
    cm[:, 6] = np.where((i >= j) & same, 0.0, NEG)
    cm[:, 7] = np.where((i > j) & same, 0.0, NEG)
    cm[:, 8] = np.where((i < j) & same, 0.0, NEG)
    cm[:, 9] = np.eye(128)
    cm[:, 10, 0:16] = (np.arange(128)[:, None] // 8) == np.arange(16)[None, :]
    c["cmask"] = cm
    k = np.arange(128)[:, None, None, None]
    rel = np.arange(20)[None, :, None, None]
    jj = np.arange(4)[None, None, :, None]
    q = np.arange(128)[None, None, None, :]
    c["mk"] = mult_m((16 + jj - rel) * 128 + q - k).reshape(128, 20, 512)
    p = np.arange(128)[:, None, None]
    kt = np.arange(16)[None, :, None]
    i = (np.arange(64) % 8)[None, None, :]
    c["msmp"] = mult_m(2048 + i - (128 * kt + p)).astype(np.float32)
    u = np.arange(16)[None, :, None]
    c["mnew"] = (mult_m(i - (p % 8)) * ((p // 8) == u)).astype(np.float32)
    return c


class Prog:
    def __init__(self, phases=("B", "ATT", "A", "MLP")):
        self.phases = phases
        self.nc = bass.Bass("TRN2", target_bir_lowering=False)
        self.ins = {}
        self.outs = {}

    def din(self, name, shape, dt=F32):
        ap = self.nc.dram_tensor(name, list(shape), dt, kind="ExternalInput").ap()
        self.ins[name] = ap
        return ap

    def dout(self, name, shape, dt=F32):
        ap = self.nc.dram_tensor(name, list(shape), dt, kind="ExternalOutput").ap()
        self.outs[name] = ap
        return ap

    def build(self):
        nc = self.nc
        xl = self.din("xl", [NLOC, D])
        w_inA = self.din("w_inA", [D, 2056])
        w_inB = self.din("w_inB", [D, 1536])
        g1bc = self.din("g1bc", [128, D])
        qgbc = self.din("qgbc", [128, 64])
        kgbc = self.din("kgbc", [128, 64])
        ident_d = self.din("ident", [128, 128])
        self.io_extra = dict(
            cmask=self.din("cmask", [128, 11, 128]),
            cwT=self.din("cwT", [128, 12, 4]),
            alogbc=self.din("alogbc", [128, 4]),
            dtbbc=self.din("dtbbc", [128, 4]),
            dngbc=self.din("dngbc", [128, 128]),
            sdelta=self.din("sdelta", [16, 4, 128, 128]),
            sconv=self.din("sconv", [16, 3, 1536]),
            dp_o=self.dout("dp_o", [4, 128, 128]),
            cp_o=self.dout("cp_o", [3, 1536]),
            ds_o=self.dout("ds_o", [16, 4, 128, 128]),
            cs_o=self.dout("cs_o", [16, 3, 1536]),
            mk=self.din("mk", [128, 20, 512]),
            msmp=self.din("msmp", [128, 16, 64]),
            mnew=self.din("mnew", [128, 16, 64]),
            pvalid=self.din("pvalid", [128, 64]),
            ck=self.din("ck", [16, 2048, 512]),
            w_o=self.din("w_o", [D, D]),
            w_up=self.din("w_up", [D, 4 * D]),
            w_down=self.din("w_down", [4 * D, D]),
            g2bc=self.din("g2bc", [128, D]),
            cv=self.din("cv", [16, 2048, 512]),
        )
        y_o = self.dout("y_o", [NOUT, D])
        k_o = self.dout("k_o", [NOUT, 512])
        v_o = self.dout("v_o", [NOUT, 512])
        self.io = dict(xl=xl, w_inA=w_inA, w_inB=w_inB, g1bc=g1bc, qgbc=qgbc, kgbc=kgbc,
                       ident=ident_d, y_o=y_o, k_o=k_o, v_o=v_o)
        self.io.update(self.io_extra)
        with contextlib.ExitStack() as st:
            self.st = st
            S = self.S = Sched(nc, st)
            self.T = lambda name, shape, dt: st.enter_context(nc.sbuf_tensor("sb_" + name, list(shape), dt))
            T = self.T
            self.bank = [st.enter_context(nc.psum_tensor("bank%d" % i, [128, 512], F32)) for i in range(8)]
            self.ident = T("ident", [128, 128], BF16)
            S.dma("pool", self.ident[:], ident_d, writes=["ident"])
            self.g1 = T("g1", [128, D], F32)
            S.dma("sp", self.g1[:], g1bc, writes=["g1"])
            self.xt = [T("xt%d" % i, [128, D], F32) for i in range(2)]
            self.junk = T("junk", [128, D], BF16)
            self.ss = [T("ss%d" % i, [128, 1], F32) for i in range(2)]
            self.rs = [T("rs%d" % i, [128, 1], F32) for i in range(2)]
            self.xn = [T("xn%d" % i, [128, D], BF16) for i in range(2)]
            self.xnT = [T("xnT%d" % i, [128, 8, 128], BF16) for i in range(2)]
            self.mixT = T("mixT", [128, 8, NOUT], BF16)
            self.KTs = T("KTs", [128, 4, 128], BF16)
            self.QTs = T("QTs", [128, 4, 128], BF16)
            self.Vbs = T("Vbs", [128, 512], BF16)
            with contextlib.ExitStack() as st1:
                T1 = lambda name, shape, dt: st1.enter_context(nc.sbuf_tensor("sb_" + name, list(shape), dt))
                self.KT = T1("KT", [128, 4, NLOC], BF16)
                self.QT = T1("QT", [128, 4, NOUT], BF16)
                self.Vb = T1("Vb", [128, 33, 512], BF16)
                if "B" in self.phases:
                    with contextlib.ExitStack() as st2:
                        self.pass_B(st2)
                S.barrier()
                if "ATT" in self.phases:
                    with contextlib.ExitStack() as st2:
                        self.attention(st2)
                S.barrier()
            if "ATT" in self.phases:
                with contextlib.ExitStack() as st2:
                    self.attention_smp(st2)
                S.barrier()
            if "A" in self.phases:
                with contextlib.ExitStack() as st2:
                    self.pass_A(st2)
                S.barrier()
            if "MLP" in self.phases:
                with contextlib.ExitStack() as st2:
                    self.mlp(st2)
            S.finish()
        return nc

    def load_x(self, t, b):
        self.S.dma("sp", self.xt[b][:], self.io["xl"][t * 128:(t + 1) * 128, :], writes=[("xt", b)])

    def norm_T(self, b, gbc, gkey, part=0, lnexp=False):
        S = self.S
        xt, ss, rs, xn, xnT = self.xt[b], self.ss[b], self.rs[b], self.xn[b], self.xnT[b]
        if part in (0, 1):
            S.op("act", lambda e: e.activation(self.junk[:], xt[:], AF.Square, accum_out=ss[:]),
                 reads=[("xt", b)], writes=["junk", ("ss", b)])
            if lnexp:
                S.op("act", lambda e: e.activation(rs[:], ss[:], AF.Ln, bias=EPS, scale=1.0 / D),
                     reads=[("ss", b)], writes=[("rs", b)])
                S.op("act", lambda e: e.activation(rs[:], rs[:], AF.Exp, scale=-0.5),
                     reads=[("rs", b)], writes=[("rs", b)])
            else:
                S.op("act", lambda e: e.activation(rs[:], ss[:], AF.Sqrt, bias=EPS, scale=1.0 / D),
                     reads=[("ss", b)], writes=[("rs", b)])
                S.op("dve", lambda e: e.reciprocal(rs[:], rs[:]), reads=[("rs", b)], writes=[("rs", b)])
            S.op("dve", lambda e: e.scalar_tensor_tensor(xn[:], xt[:], rs[:, 0:1], gbc[:], ALU.mult, ALU.mult),
                 reads=[("xt", b), ("rs", b), gkey], writes=[("xn", b)])
        if part == 1:
            return
        for kc in range(8):
            bk = self.bank[kc // 4]
            S.op("pe", lambda e: e.matmul(bk[:, (kc % 4) * 128:(kc % 4 + 1) * 128],
                                          xn[:, kc * 128:(kc + 1) * 128], self.ident[:],
                                          start=True, stop=True),
                 reads=[("xn", b), "ident"], writes=[("bank", kc // 4)])
        S.op("act", lambda e: e.copy(xnT[:, 0:4, :].rearrange("p a b -> p (a b)"), self.bank[0][:]),
             reads=[("bank", 0)], writes=[("xnT", b)])
        S.op("dve", lambda e: e.tensor_copy(xnT[:, 4:8, :].rearrange("p a b -> p (a b)"), self.bank[1][:]),
             reads=[("bank", 1)], writes=[("xnT", b)])

    def pass_B(self, st2):
        nc, S, T = self.nc, self.S, self.T
        T2 = lambda name, shape, dt: st2.enter_context(nc.sbuf_tensor("sb_" + name, list(shape), dt))
        wB = T2("wB", [128, 8, 1536], BF16)
        wsrc = self.io["w_inB"].rearrange("(kc p) n -> p kc n", p=128)
        for kc in range(8):
            S.dma("pool", wB[:, kc, :], wsrc[:, kc, :], writes=[("wB", kc)])
        qg = T2("qg", [128, 64], F32)
        kg = T2("kg", [128, 64], F32)
        S.dma("sp", qg[:], self.io["qgbc"], writes=["qg"])
        S.dma("sp", kg[:], self.io["kgbc"], writes=["kg"])
        sq = T2("sqB", [128, 512], F32)
        ss8 = [T2("ss8_%d" % i, [128, 8], F32) for i in range(2)]
        nrm = [T2("nrm%d" % i, [128, 512], F32) for i in range(4)]
        nbf = [T2("nbf%d" % i, [128, 512], BF16) for i in range(2)]
        v32 = [T2("v32_%d" % i, [128, 512], F32) for i in range(2)]
        wkeys = [("wB", kc) for kc in range(8)]
        self.load_x(0, 0)

        def stage1(t):
            if t + 1 < 33:
                self.load_x(t + 1, (t + 1) % 2)
            self.norm_T(t % 2, self.g1, "g1")

        def tileB(t):
            b = t % 2
            xnT = self.xnT[b]
            own = t >= 16
            groups = [0, 1, 2] if own else [1, 2]
            for g in groups:
                bk = self.bank[2 + g]
                for kc in range(8):
                    S.op("pe", lambda e: e.matmul(bk[:], xnT[:, kc, :], wB[:, kc, g * 512:(g + 1) * 512],
                                                  start=(kc == 0), stop=(kc == 7)),
                         reads=[("xnT", b), ("wB", kc)], writes=[("bank", 2 + g)])
            yield
            orow = (t - 16) * 128
            vb = self.bank[4]
            vv = v32[t % 2]
            if own:
                S.op("act", lambda e: e.copy(vv[:], vb[:]), reads=[("bank", 4)], writes=[("v32", t % 2)])
                S.dma("sp", self.io["v_o"][orow:orow + 128, :], vv[:], reads=[("v32", t % 2)])
            S.op("dve", lambda e: e.tensor_copy(self.Vb[:, t, :], vb[:]), reads=[("bank", 4)], writes=[("Vb", t)])
            for g in groups[:-1]:
                bk = self.bank[2 + g]
                gi = g
                s8 = ss8[gi]
                nr = nrm[2 * gi + (t % 2)]
                nkey = ("nrm", 2 * gi + (t % 2))
                S.op("act", lambda e: e.activation(sq[:], bk[:], AF.Square),
                     reads=[("bank", 2 + g)], writes=["sqB"])
                S.op("dve", lambda e: e.tensor_reduce(s8[:], sq[:].rearrange("p (h d) -> p h d", d=64),
                                                      AX.X, ALU.add),
                     reads=["sqB"], writes=[("ss8", gi)])
                S.op("act", lambda e: e.activation(s8[:], s8[:], AF.Sqrt, bias=EPS, scale=1.0 / 64),
                     reads=[("ss8", gi)], writes=[("ss8", gi)])
                S.op("dve", lambda e: e.reciprocal(s8[:], s8[:]), reads=[("ss8", gi)], writes=[("ss8", gi)])
                S.op("dve", lambda e: e.tensor_tensor(
                    nr[:].rearrange("p (h d) -> p h d", d=64),
                    bk[:].rearrange("p (h d) -> p h d", d=64),
                    s8[:].unsqueeze(2).to_broadcast([128, 8, 64]), ALU.mult),
                    reads=[("bank", 2 + g), ("ss8", gi)], writes=[nkey])
                gt = qg if gi == 0 else kg
                S.op("pool", lambda e: e.tensor_tensor(
                    nr[:].rearrange("p (h d) -> p h d", d=64),
                    nr[:].rearrange("p (h d) -> p h d", d=64),
                    gt[:].unsqueeze(1).to_broadcast([128, 8, 64]), ALU.mult),
                    reads=[nkey, "qg" if gi == 0 else "kg"], writes=[nkey])
                if gi == 1 and own:
                    S.dma("sp", self.io["k_o"][orow:orow + 128, :], nr[:], reads=[nkey])
                yield
                nb = nbf[gi]
                S.op("act", lambda e: e.copy(nb[:], nr[:]), reads=[nkey], writes=[("nbf", gi)])
                tb = self.bank[5 + gi]
                for c in range(4):
                    S.op("pe", lambda e: e.matmul(tb[:, c * 128:(c + 1) * 128], nb[:, c * 128:(c + 1) * 128],
                                                  self.ident[:], start=True, stop=True),
                         reads=[("nbf", gi), "ident"], writes=[("bank", 5 + gi)])
                if gi == 0:
                    dst = self.QT[:, :, orow:orow + 128]
                    dkey = ("QT", t)
                else:
                    dst = self.KT[:, :, t * 128:(t + 1) * 128]
                    dkey = ("KT", t)
                S.op("dve", lambda e: e.tensor_copy(dst, tb[:].rearrange("p (c n) -> p c n", n=128)),
                     reads=[("bank", 5 + gi)], writes=[dkey])
                yield

        self.load_x(1, 1)
        self.norm_T(0, self.g1, "g1")
        for t in range(33):
            if t + 1 < 33:
                self.norm_T((t + 1) % 2, self.g1, "g1", part=1)
            g = tileB(t)
            next(g)
            if t + 1 < 33:
                self.norm_T((t + 1) % 2, self.g1, "g1", part=2)
            if t + 2 < 33:
                self.load_x(t + 2, t % 2)
            for _ in g:
                pass

    def attention(self, st2):
        nc, S, io = self.nc, self.S, self.io
        T2 = lambda name, shape, dt: st2.enter_context(nc.sbuf_tensor("sb_" + name, list(shape), dt))
        B = self.bank
        identb = self.ident
        MK = T2("MK", [128, 20, 512], BF16)
        for r in range(0, 20, 5):
            S.dma("pool", MK[:, r:r + 5, :], io["mk"][:, r:r + 5, :], writes=[("MK", r // 5)])
        MKK = [("MK", i) for i in range(4)]
        VO = [T2("VO%d" % i, [128, 32, 128], BF16) for i in range(2)]
        pv = T2("pvalid", [128, 64], F32)
        S.dma("sp", pv[:], io["pvalid"], writes=["pv"])
        for par in range(2):
            doff = 64 if par == 0 else 0
            S.op("pool", lambda e: e.tensor_copy(VO[par][:, 0:16, doff:doff + 64],
                                                 pv[:].unsqueeze(1).to_broadcast([128, 16, 64])),
                 reads=["pv"], writes=[("VO", par)])
            S.op("pool", lambda e: e.memset(VO[par][:, 16:32, doff:doff + 64], 1.0), writes=[("VO", par)])
        NSB = 6
        GRP = 3
        SBK = [0, 1, 2, 3, 6, 7]
        Eb = [T2("Eb%d" % i, [128, 512], BF16) for i in range(NSB)]
        Pb = [T2("Pb%d" % i, [128, 512], BF16) for i in range(NSB)]
        rD = [T2("rD%d" % i, [128, 512], F32) for i in range(2)]
        VK = [("Vb", t) for t in range(32)]
        its = []
        for h in range(8):
            for sb in range(4):
                for rel in range(20):
                    its.append((h, sb, rel))

        def geom(i):
            h, sb, rel = its[i]
            c, par = h // 2, h % 2
            po = par * 64
            kt = 4 * sb + rel
            j0, j1 = max(0, rel - 16), min(3, rel)
            cs = slice(128 * j0, 128 * (j1 + 1))
            qs = slice(4 * sb * 128 + 128 * j0, 4 * sb * 128 + 128 * (j1 + 1))
            return h, sb, rel, c, par, po, kt, j0, j1, cs, qs

        def rwS(i):
            h, sb, rel, c, par, po, kt, j0, j1, cs, qs = geom(i)
            return ([("KT", kt)] + [("QT", 16 + 4 * sb + j) for j in range(j0, j1 + 1)], [("bank", SBK[i % NSB])])

        def rwO(i):
            h, sb, rel, c, par, po, kt, j0, j1, cs, qs = geom(i)
            return ([("VO", par), ("Pb", i % NSB)], [("bank", 4 + (h * 4 + sb) % 2)])

        def emitS(i):
            h, sb, rel, c, par, po, kt, j0, j1, cs, qs = geom(i)
            sbk = i % NSB
            rd, wr = rwS(i)
            S.op("pe", lambda e: e.matmul(B[SBK[sbk]][:, cs], self.KT[po:po + 64, c, kt * 128:(kt + 1) * 128],
                                          self.QT[po:po + 64, c, qs], start=True, stop=True),
                 reads=rd, writes=wr)

        def emitE(i):
            h, sb, rel, c, par, po, kt, j0, j1, cs, qs = geom(i)
            sbk = i % NSB
            S.op("act", lambda e: e.activation(Eb[sbk][:, cs], B[SBK[sbk]][:, cs], AF.Exp, scale=0.125),
                 reads=[("bank", SBK[sbk])], writes=[("Eb", sbk)])
            S.op("dve",
                 lambda e: e.tensor_tensor(Pb[sbk][:, cs], Eb[sbk][:, cs], MK[:, rel, cs], ALU.mult),
                 reads=[("Eb", sbk)] + MKK, writes=[("Pb", sbk)])

        def emitO(i):
            h, sb, rel, c, par, po, kt, j0, j1, cs, qs = geom(i)
            sbk = i % NSB
            ob = 4 + (h * 4 + sb) % 2
            if sb == 0 and rel == 0:
                voff = 0 if par == 0 else 64
                S.op("pool", lambda e: e.tensor_copy(VO[par][:, :, voff:voff + 64], self.Vb[:, 0:32, h * 64:(h + 1) * 64]),
                     reads=VK, writes=[("VO", par)])
            S.op("pe", lambda e: e.matmul(B[ob][:, cs], VO[par][:, kt, :], Pb[sbk][:, cs],
                                          start=(rel == 0), stop=(rel == 19), skip_group_check=True),
                 reads=[("VO", par), ("Pb", sbk)], writes=[("bank", ob)])
            if rel == 19:
                dlo = 64 - po
                rd = rD[ob - 4]
                S.op("act", lambda e: e.activation(rd[po:po + 64, :], B[ob][dlo:dlo + 64, :], AF.Ln),
                     reads=[("bank", ob)], writes=[("rD", ob)])
                S.op("act", lambda e: e.activation(rd[po:po + 64, :], rd[po:po + 64, :], AF.Exp, scale=-1.0),
                     reads=[("rD", ob)], writes=[("rD", ob)])
                S.op("dve", lambda e: e.tensor_tensor(self.mixT[po:po + 64, 4 + c, 4 * sb * 128:4 * sb * 128 + 512],
                                                      B[ob][po:po + 64, :], rd[po:po + 64, :], ALU.mult),
                     reads=[("bank", ob), ("rD", ob)], writes=[("mixT", "b", h, sb)])

        n = len(its)
        groups = [list(range(i, min(i + GRP, n))) for i in range(0, n, GRP)]

        def emitSg(g):
            S.prewait("pe", [rwS(i) for i in groups[g]])
            for i in groups[g]:
                emitS(i)

        def emitOg(g):
            for i in groups[g]:
                h, sb, rel = its[i]
                if sb == 0 and rel == 0:
                    break
            else:
                S.prewait("pe", [rwO(i) for i in groups[g]])
            for i in groups[g]:
                emitO(i)

        ng = len(groups)
        emitSg(0)
        for g in range(ng):
            if g + 1 < ng:
                emitSg(g + 1)
            for i in groups[g]:
                emitE(i)
            if g >= 1:
                emitOg(g - 1)
        emitOg(ng - 1)
        S.op("pool", lambda e: e.tensor_copy(self.KTs[:], self.KT[:, :, NPRE + NOWN:NLOC]), reads=[("KT", 32)], writes=["KTs"])
        S.op("pool", lambda e: e.tensor_copy(self.QTs[:], self.QT[:, :, NOWN:NOUT]), reads=[("QT", 32)], writes=["QTs"])
        S.op("pool", lambda e: e.tensor_copy(self.Vbs[:], self.Vb[:, 32, :]), reads=[("Vb", 32)], writes=["Vbs"])

    def attention_smp(self, st2):
        nc, S, io = self.nc, self.S, self.io
        T2 = lambda name, shape, dt: st2.enter_context(nc.sbuf_tensor("sb_" + name, list(shape), dt))
        B = self.bank
        identb = self.ident
        Kc = [T2("Kc%d" % i, [128, 16, 512], BF16) for i in range(2)]
        Vc = [T2("Vc%d" % i, [128, 16, 512], BF16) for i in range(2)]
        KcT = [T2("KcT%d" % i, [128, 4, 2048], BF16) for i in range(2)]
        msmp = T2("msmp", [128, 16, 64], BF16)
        mnew = T2("mnew", [128, 16, 64], BF16)
        S.dma("pool", msmp[:], io["msmp"], writes=["msmp"])
        S.dma("pool", mnew[:], io["mnew"], writes=["mnew"])
        onesb = T2("onesb2", [128, 128], BF16)
        S.op("dve", lambda e: e.memset(onesb[:], 1.0), writes=["onesb2"])
        Qblk = T2("Qblk", [128, 4, 16], BF16)
        S.op("dve", lambda e: e.memset(Qblk[:], 0.0), writes=["Qblk"])
        Es = T2("Es", [128, 17, 64], BF16)
        Ps = T2("Ps", [128, 17, 64], BF16)
        rDs = T2("rDs", [128, 64], F32)

        def load(u):
            b = u % 2
            S.dma("pool", Kc[b][:], io["ck"][u].rearrange("(t p) f -> p t f", p=128), writes=[("Kc", b)])
            S.dma("pool", Vc[b][:], io["cv"][u].rearrange("(t p) f -> p t f", p=128), writes=[("Vc", b)])

        load(0)
        for u in range(16):
            b = u % 2
            if u + 1 < 16:
                load(u + 1)
            for kt in range(16):
                tb = 5 + kt % 2
                for c in range(4):
                    S.op("pe", lambda e: e.matmul(B[tb][:, c * 128:(c + 1) * 128], Kc[b][:, kt, c * 128:(c + 1) * 128],
                                                  identb[:], start=True, stop=True),
                         reads=[("Kc", b), "ident"], writes=[("bank", tb)])
                dst = KcT[b][:, :, kt * 128:(kt + 1) * 128]
                src = B[tb][:].rearrange("p (c n) -> p c n", c=4)
                if kt % 2 == 0:
                    S.op("act", lambda e: e.copy(dst, src), reads=[("bank", tb)], writes=[("KcT", b, kt)])
                else:
                    S.op("dve", lambda e: e.tensor_copy(dst, src), reads=[("bank", tb)], writes=[("KcT", b, kt)])
            S.op("pool", lambda e: e.tensor_copy(Qblk[0:64, :, 0:8], self.QTs[0:64, :, 8 * u:8 * u + 8]),
                 reads=["QTs"], writes=["Qblk"])
            S.op("pool", lambda e: e.tensor_copy(Qblk[64:128, :, 8:16], self.QTs[64:128, :, 8 * u:8 * u + 8]),
                 reads=["QTs"], writes=["Qblk"])
            for kt in range(16):
                sbk = kt // 8
                for c in range(4):
                    col = (kt % 8) * 64 + c * 16
                    S.op("pe", lambda e: e.matmul(B[sbk][:, col:col + 16], KcT[b][:, c, kt * 128:(kt + 1) * 128],
                                                  Qblk[:, c, :], start=True, stop=True),
                         reads=[("KcT", b, kt), "Qblk"], writes=[("bank", sbk)])
            for c in range(4):
                S.op("pe", lambda e: e.matmul(B[2][:, c * 16:(c + 1) * 16], self.KTs[:, c, :], Qblk[:, c, :],
                                              start=True, stop=True), reads=["KTs", "Qblk"], writes=[("bank", 2)])
            for sbk in range(2):
                S.op("act", lambda e: e.activation(Es[:, 8 * sbk:8 * sbk + 8, :].rearrange("p t n -> p (t n)"),
                                                   B[sbk][:], AF.Exp, scale=0.125),
                     reads=[("bank", sbk)], writes=[("Es", sbk)])
            S.op("act", lambda e: e.activation(Es[:, 16, :], B[2][:, 0:64], AF.Exp, scale=0.125),
                 reads=[("bank", 2)], writes=[("Es", 2)])
            S.op("dve", lambda e: e.tensor_tensor(Ps[:, 0:16, :], Es[:, 0:16, :], msmp[:], ALU.mult),
                 reads=[("Es", 0), ("Es", 1), "msmp"], writes=["Ps"])
            S.op("dve", lambda e: e.tensor_tensor(Ps[:, 16, :], Es[:, 16, :], mnew[:, u, :], ALU.mult),
                 reads=[("Es", 2), "mnew"], writes=["Ps"])
            first = True
            for kt in range(17):
                for c in range(4):
                    lhsT = Vc[b][:, kt, c * 128:(c + 1) * 128] if kt < 16 else self.Vbs[:, c * 128:(c + 1) * 128]
                    S.op("pe", lambda e: e.matmul(B[3][:, c * 16:(c + 1) * 16], lhsT, Ps[:, kt, c * 16:(c + 1) * 16],
                                                  start=first, stop=False, skip_group_check=True),
                         reads=[("Vc", b), "Vbs", "Ps"], writes=[("bank", 3)])
                    first = False
                S.op("pe", lambda e: e.matmul(B[3][:, 64:128], onesb[:], Ps[:, kt, :], start=False, stop=(kt == 16),
                                              skip_group_check=True),
                     reads=["onesb2", "Ps"], writes=[("bank", 3)])
            S.op("dve", lambda e: e.reciprocal(rDs[:], B[3][:, 64:128]), reads=[("bank", 3)], writes=["rDs"])
            for hh in range(2):
                rows = slice(hh * 64, hh * 64 + 64)
                S.op("dve", lambda e: e.tensor_tensor(
                    self.mixT[rows, 4:8, NOWN + 8 * u:NOWN + 8 * u + 8],
                    B[3][rows, 0:64].rearrange("p (c x) -> p c x", c=4)[:, :, hh * 8:hh * 8 + 8],
                    rDs[rows, :].rearrange("p (c x) -> p c x", c=4)[:, :, hh * 8:hh * 8 + 8], ALU.mult),
                    reads=[("bank", 3), "rDs"], writes=[("mixT", "s", u, hh)])

    def mlp(self, st2):
        nc, S, io = self.nc, self.S, self.io
        T2 = lambda name, shape, dt: st2.enter_context(nc.sbuf_tensor("sb_" + name, list(shape), dt))
        B = self.bank
        identb = self.ident
        wup = T2("wup", [128, 8, 4096], BF16)
        wdn = T2("wdn", [128, 32, 1024], BF16)
        st3 = contextlib.ExitStack()
        wo = st3.enter_context(nc.sbuf_tensor("sb_wo", [128, 8, 1024], BF16))
        wosrc = io["w_o"].rearrange("(kc p) n -> p kc n", p=128)
        for kc in range(8):
            S.dma("pool", wo[:, kc, :], wosrc[:, kc, :], writes=[("wo", kc)])
        g2 = self.g1
        S.dma("sp", g2[:], io["g2bc"], writes=["g1"])
        usrc = io["w_up"].rearrange("(kc p) n -> p kc n", p=128)
        dsrc = io["w_down"].rearrange("(f p) n -> p f n", p=128)
        for kc in range(8):
            S.dma("pool", wup[:, kc, :], usrc[:, kc, :], writes=[("wup", kc)])
        for f in range(0, 32, 4):
            S.dma("pool", wdn[:, f:f + 4, :], dsrc[:, f:f + 4, :], writes=[("wdn", f // 4)])
        h1s = self.xt
        tiles = list(range(16, 33))

        def mixkeys(t):
            if t < 32:
                i = t - 16
                return [("mixT", t)] + [("mixT", "b", h, i // 4) for h in range(8)]
            return [("mixT", 32)] + [("mixT", "s", u, hh) for u in range(16) for hh in range(2)]

        self.load_x(tiles[0], 0)

        def stage3a(n, t):
            b = n % 2
            if n + 1 < len(tiles):
                self.load_x(tiles[n + 1], (n + 1) % 2)
            orow = (t - 16) * 128
            cols = slice(orow, orow + 128)
            mk = mixkeys(t)
            for half in range(2):
                for kc in range(8):
                    S.op("pe", lambda e: e.matmul(B[2 + half][:], self.mixT[:, kc, cols], wo[:, kc, half * 512:(half + 1) * 512],
                                                  start=(kc == 0), stop=(kc == 7)),
                         reads=mk + [("wo", kc)], writes=[("bank", 2 + half)])
            h1 = h1s[b]
            for half in range(2):
                hsl = slice(half * 512, (half + 1) * 512)
                S.op("dve", lambda e: e.tensor_tensor(h1[:, hsl], self.xt[b][:, hsl], B[2 + half][:], ALU.add),
                     reads=[("bank", 2 + half), ("xt", b)], writes=[("xt", b)])
            S.dma("sp", io["y_o"][orow:orow + 128, :], h1[:], reads=[("xt", b)], writes=[("h1d", t)])
            ss, rs, xn = self.ss[b], self.rs[b], self.xn[b]
            S.op("act", lambda e: e.activation(self.junk[:], h1[:], AF.Square, accum_out=ss[:]),
                 reads=[("xt", b)], writes=["junk", ("ss", b)])
            S.op("act", lambda e: e.activation(rs[:], ss[:], AF.Sqrt, bias=EPS, scale=1.0 / D),
                 reads=[("ss", b)], writes=[("rs", b)])
            S.op("dve", lambda e: e.reciprocal(rs[:], rs[:]), reads=[("rs", b)], writes=[("rs", b)])
            S.op("dve", lambda e: e.scalar_tensor_tensor(xn[:], h1[:], rs[:, 0:1], g2[:], ALU.mult, ALU.mult),
                 reads=[("xt", b), ("rs", b), "g1"], writes=[("xn", b)])
            yield
            for kc in range(8):
                bk = B[kc // 4]
                S.op("pe", lambda e: e.matmul(bk[:, (kc % 4) * 128:(kc % 4 + 1) * 128],
                                              xn[:, kc * 128:(kc + 1) * 128], identb[:], start=True, stop=True),
                     reads=[("xn", b), "ident"], writes=[("bank", kc // 4)])
            S.op("act", lambda e: e.copy(self.mixT[:, 0:4, cols], B[0][:].rearrange("p (a n) -> p a n", a=4)),
                 reads=[("bank", 0)], writes=mk)
            S.op("dve", lambda e: e.tensor_copy(self.mixT[:, 4:8, cols], B[1][:].rearrange("p (a n) -> p a n", a=4)),
                 reads=[("bank", 1)], writes=mk)

        gens3a = [stage3a(n, t) for n, t in enumerate(tiles)]
        next(gens3a[0])
        for n in range(len(tiles)):
            if n + 1 < len(tiles):
                next(gens3a[n + 1])
            for _ in gens3a[n]:
                pass
        S.barrier()
        st3.close()
        hidT = T2("hidT", [128, 32, 128], BF16)
        rl = [T2("rl%d" % i, [128, 512], BF16) for i in range(2)]
        for n, t in enumerate(tiles):
            b = n % 2
            orow = (t - 16) * 128
            cols = slice(orow, orow + 128)
            mk = mixkeys(t)
            h1 = h1s[b]
            S.dma("sp", h1[:], io["y_o"][orow:orow + 128, :], reads=[("h1d", t)], writes=[("xt", b)])
            for fg in range(8):
                bk = 2 + fg % 2
                for q in range(4):
                    f = 4 * fg + q
                    for kc in range(8):
                        S.op("pe", lambda e: e.matmul(B[bk][:, q * 128:(q + 1) * 128], wup[:, kc, f * 128:(f + 1) * 128],
                                                      self.mixT[:, kc, cols], start=(kc == 0), stop=(kc == 7)),
                             reads=mk + [("wup", kc)], writes=[("bank", bk)])
                r = rl[fg % 2]
                S.op("act", lambda e: e.activation(r[:], B[bk][:], AF.Relu), reads=[("bank", bk)], writes=[("rl", fg % 2)])
                S.op("pool" if fg % 2 else "dve",
                     lambda e: e.tensor_tensor(hidT[:, 4 * fg:4 * fg + 4, :].rearrange("p a n -> p (a n)"), r[:], r[:], ALU.mult),
                     reads=[("rl", fg % 2)], writes=[("hidT", fg)])
            for half in range(2):
                for f in range(32):
                    S.op("pe", lambda e: e.matmul(B[4 + half][:], hidT[:, f, :], wdn[:, f, half * 512:(half + 1) * 512],
                                                  start=(f == 0), stop=(f == 31)),
                         reads=[("hidT", f // 4), ("wdn", f // 4)], writes=[("bank", 4 + half)])
            for half in range(2):
                hsl = slice(half * 512, (half + 1) * 512)
                S.op("dve", lambda e: e.tensor_tensor(h1[:, hsl], h1[:, hsl], B[4 + half][:], ALU.add),
                     reads=[("bank", 4 + half), ("xt", b)], writes=[("xt", b)])
            S.dma("sp", io["y_o"][orow:orow + 128, :], h1[:], reads=[("xt", b), ("h1d", t)], writes=[("yd", t)])

    def pass_A(self, st2):
        nc, S, io = self.nc, self.S, self.io
        T2 = lambda name, shape, dt: st2.enter_context(nc.sbuf_tensor("sb_" + name, list(shape), dt))
        B = self.bank
        def bq(i, q=None):
            return [("b", i, k) for k in range(4)] if q is None else [("b", i, q)]
        wA = T2("wA", [128, 8, 2056], BF16)
        wsrc = io["w_inA"].rearrange("(kc p) n -> p kc n", p=128)
        for kc in range(8):
            S.dma("pool", wA[:, kc, :], wsrc[:, kc, :], writes=[("wA", kc)])
        cm = T2("cm", [128, 11, 128], F32)
        S.dma("sp", cm[:], io["cmask"], writes=["cm"])
        cw = T2("cw", [128, 12, 4], F32)
        S.dma("sp", cw[:], io["cwT"], writes=["cw"])
        nea = T2("nea", [128, 4], F32)
        dtb = T2("dtb", [128, 4], F32)
        dng = T2("dng", [128, 128], F32)
        S.dma("sp", nea[:], io["alogbc"], writes=["nea"])
        S.dma("sp", dtb[:], io["dtbbc"], writes=["dtb"])
        S.dma("sp", dng[:], io["dngbc"], writes=["dng"])
        S.op("act", lambda e: e.activation(nea[:], nea[:], AF.Exp), reads=["nea"], writes=["nea"])
        S.op("dve", lambda e: e.tensor_scalar(nea[:], nea[:], -1.0, None, ALU.mult), reads=["nea"], writes=["nea"])
        onesf = T2("onesf", [128, 128], F32)
        onesb = T2("onesb", [128, 128], BF16)
        S.op("dve", lambda e: e.memset(onesf[:], 1.0), writes=["onesf"])
        S.op("dve", lambda e: e.memset(onesb[:], 1.0), writes=["onesb"])
        identf = cm[:, 9, :]
        identb = self.ident
        uT = [T2("uT%d" % i, [128, 12, 131], F32) for i in range(2)]
        acc = T2("acc", [128, 12, 128], F32)
        ctmp = T2("ctmp", [128, 128], F32)
        stP = contextlib.ExitStack()
        TP = lambda name, shape, dt: stP.enter_context(nc.sbuf_tensor("sb_" + name, list(shape), dt))
        PA = lambda i: T2 if i == 0 else TP
        sqc = T2("sqc", [128, 8, 128], BF16)
        rst = T2("rst", [128, 8, 128], F32)
        (BAS0, BAS1, EB, BETA, LNEB, AD, G_, GL_, EG, EGLG, EGL, NBG, NEGG, GLN, SSO, RSO) = range(16)
        diagG = T2("diagG", [128, 4, 128], F32)
        diagL = T2("diagL", [128, 4, 128], F32)
        tU = T2("tU", [128, 4, 128], F32)
        tS = T2("tS", [128, 4, 128], F32)
        tL = T2("tL", [128, 4, 128], F32)
        E1, E2, E3 = tU, tS, tL
        Mm = [T2("Mm%d" % i, [128, 4, 128], BF16) for i in range(2)]
        MTr = [T2("MT%d" % i, [128, 4, 128], BF16) for i in range(2)]
        v1f = T2("v1f", [128, 4, 128], F32)
        R32 = T2("R32", [128, 4, 128], F32)
        t1 = T2("t1", [128, 4, 128], F32)
        v1b = T2("v1b", [128, 4, 128], BF16)
        r1b = T2("r1b", [128, 4, 128], BF16)
        S32 = T2("S32", [128, 4, 128], F32)
        Sbf = T2("Sbf", [128, 4, 128], BF16)
        Rr = T2("Rr", [128, 4, 128], BF16)
        vnew = T2("vnew", [128, 4, 128], BF16)
        tmpo = T2("tmpo", [128, 4, 128], F32)
        o32 = T2("o32", [128, 4, 128], F32)
        sqo = t1[:].rearrange("p h n -> p (h n)")
        T1K = [("t1", h) for h in range(4)]
        obf = T2("obf", [128, 512], BF16)
        cps = acc[:].rearrange("p c n -> p (c n)")[0:48, :]
        ACCK = [("acc", ch) for ch in range(12)]
        stok = uT[0][:].rearrange("p c n -> p (c n)")[0:48, 0:1536]
        cTs, kds, vbs, sms, PTs, LT0s, qkTs, zss = [], [], [], [], [], [], [], []

        def alloc_par(i):
            A_ = PA(i)
            cTs.append(A_("cT%d" % i, [128, 12, 128], BF16))
            kds.append(A_("kd%d" % i, [128, 4, 128], BF16))
            vbs.append(A_("vb%d" % i, [128, 4, 128], BF16))
            sms.append(A_("sm%d" % i, [128, 16, 4], F32))
            PTs.append([A_("PT%d_%d" % (i, j), [128, 4, 128], BF16) for j in range(2)])
            LT0s.append(A_("LT0_%d" % i, [128, 4, 128], BF16))
            qkTs.append(A_("qkT%d" % i, [128, 4, 128], BF16))
            zss.append(A_("zs%d" % i, [128, 512], F32))

        alloc_par(0)
        uS = cs3 = Ssm32b = Ssmbb = kSTs = None
        gsel = T2("gsel", [128, 4, 16], F32)
        eglb = T2("eglb", [128, 4, 16], F32)
        vblk = acc[:].rearrange("p c n -> p (c n)").bitcast(BF16)[:, 0:2048].rearrange("p (u v) -> p u v", u=16)
        S.op("dve", lambda e: e.memset(S32[:], 0.0), writes=[("S32", h) for h in range(4)])
        S.op("dve", lambda e: e.memset(Sbf[:], 0.0), writes=[("Sbf", h) for h in range(4)])
        S.op("dve", lambda e: e.memset(uT[1][:, :, 128:131], 0.0), writes=[("uT", 1)])
        def load_smp_state(h):
            S.dma("sp", Ssm32b[h % 2][:], io["sdelta"][:, h].rearrange("u k v -> k u v"), writes=[("Ssm32", h % 2)])
            S.dma("pool", Ssmbb[h % 2][:], io["sdelta"][:, h].rearrange("u k v -> k u v"), writes=[("Ssmb", h % 2)])

        def tileA(t):
            p = t % 2
            N = lambda k: (k, "par", p)
            cT, kd, vb, sm, PT, qkT, zs = cTs[p], kds[p], vbs[p], sms[p], PTs[p], qkTs[p], zss[p]
            MT = [MTr[0], MTr[1], LT0s[p]]
            MK_ = lambda i: N(("MT", 2)) if i == 2 else ("MT", i)

            def sc(i, h=None):
                return sm[:, i, :] if h is None else sm[:, i, h:h + 1]

            if t + 2 < 33:
                self.load_x(t + 2, t % 2)
            if t + 1 < 33:
                self.norm_T((t + 1) % 2, self.g1, "g1", part=1, lnexp=True)
            yield
            mode = "pre" if t < 16 else ("own" if t < 32 else "smp")
            own = mode != "pre"
            smp = mode == "smp"
            b = t % 2
            xnT = self.xnT[b]
            chs = list(range(12)) if own else list(range(4, 12))
            ub = uT[b]
            orow = (t - 16) * 128
            msk = (1, 2, 6, 7, 8) if smp else (0, None, 3, 4, 5)
            tri = cm[:, msk[0], :]
            glones = cm[:, 2, :] if smp else onesf[:]
            nm_ui, nm_us, nm_ls = cm[:, msk[2], :], cm[:, msk[3], :], cm[:, msk[4], :]
            nlev = 3 if smp else 7
            chs_proj = list(range(12)) if t == 15 else chs
            for ch in chs_proj:
                bi = (2 + ch // 4) if smp else (2 + (ch // 4) % 2)
                col = (ch % 4) * 128
                for kc in range(8):
                    S.op("pe", lambda e: e.matmul(B[bi][:, col:col + 128], wA[:, kc, ch * 128:(ch + 1) * 128],
                                                  xnT[:, kc, :], start=(kc == 0), stop=(kc == 7)),
                         reads=[("xnT", b), ("wA", kc)], writes=bq(bi, ch % 4))
                if not smp and ch % 4 == 3:
                    gi = ch // 4
                    S.op("act", lambda e: e.copy(ub[:, 4 * gi:4 * gi + 4, 3:131],
                                                 B[bi][:].rearrange("p (c n) -> p c n", c=4)),
                         reads=bq(bi), writes=[("uT", b)])
                    S.op("dve", lambda e: e.tensor_copy(ubf[:, 4 * gi:4 * gi + 4, 3:131],
                                                        B[bi][:].rearrange("p (c n) -> p c n", c=4)),
                         reads=bq(bi), writes=[("ubf", gi)])
            if t + 1 < 33:
                self.norm_T((t + 1) % 2, self.g1, "g1", part=2)
            if smp:
                S.dma("sp", stok, io["sconv"].rearrange("u r c -> (u r) c"), writes=[("uT", 0)])
                load_smp_state(0)
                load_smp_state(1)
                for ch in range(12):
                    bi = 5 + ch // 6
                    col = (ch % 6) * 48
                    S.op("pe", lambda e: e.matmul(B[bi][:, col:col + 48], stok[:, ch * 128:(ch + 1) * 128],
                                                  identf[0:48, 0:48], start=True, stop=True),
                         reads=[("uT", 0), "cm"], writes=bq(bi))
                for gi in range(2):
                    S.op("dve", lambda e: e.tensor_copy(
                        uS[:, 6 * gi:6 * gi + 6, :, 0:3],
                        B[5 + gi][:, 0:288].rearrange("p (c u r) -> p c u r", c=6, u=16)),
                        reads=bq(5 + gi), writes=["uS"])
                for gi in range(3):
                    S.op("act", lambda e: e.copy(
                        uS[:, 4 * gi:4 * gi + 4, :, 3:11],
                        B[2 + gi][:].rearrange("p (c u r) -> p c u r", c=4, u=16)),
                        reads=bq(2 + gi), writes=["uS"])
            else:
                up = uT[1 - b]
                S.op("dve", lambda e: e.tensor_copy(ub[:, :, 0:3], up[:, :, 128:131]),
                     reads=[("uT", 1 - b)], writes=[("uT", b)])
                S.op("dve", lambda e: e.tensor_copy(ubf[:, :, 0:3], up[:, :, 128:131]),
                     reads=[("uT", 1 - b)], writes=[("ubf", "c")])
            for kc in range(8):
                S.op("pe", lambda e: e.matmul(B[0][:, 0:8], xnT[:, kc, :], wA[:, kc, 2048:2056],
                                              start=(kc == 0), stop=(kc == 7)),
                     reads=[("xnT", b), ("wA", kc)], writes=bq(0, 0))
            if own:
                for kc in range(8):
                    S.op("pe", lambda e: e.matmul(B[1][:], xnT[:, kc, :], wA[:, kc, 1536:2048],
                                                  start=(kc == 0), stop=(kc == 7)),
                         reads=[("xnT", b), ("wA", kc)], writes=bq(1))
            yield
            if not smp:
                for gi in sorted(set(ch // 4 for ch in chs)):
                    bi = 2 + gi % 2
                    for ch in range(4 * gi, 4 * gi + 4):
                        for i in range(4):
                            S.op("pe", lambda e: e.matmul(B[bi][:, (ch % 4) * 128:(ch % 4 + 1) * 128], Dg[:, 4 * ch + i, :],
                                                          ubf[:, ch, i:i + 128], start=(i == 0), stop=(i == 3)),
                                 reads=["Dg", ("ubf", gi), ("ubf", "c")], writes=bq(bi, ch % 4))
                    S.op("act", lambda e: e.activation(cT[:, 4 * gi:4 * gi + 4, :].rearrange("p c n -> p (c n)"), B[bi][:], AF.Silu),
                         reads=bq(bi), writes=[N("cT")])
            for n, ch in enumerate(chs if smp else []):
                eng = "dve"
                for i in range(4):
                    if smp:
                        src = uS[:, ch, :, i:i + 8]
                        dst = acc[:, ch, :].rearrange("p (u r) -> p u r", u=16)
                        skey = "uS"
                    else:
                        src = ub[:, ch, i:i + 128]
                        dst = acc[:, ch, :]
                        skey = ("uT", b)
                    if i == 0:
                        S.op(eng, lambda e: e.tensor_scalar(dst, src, cw[:, ch, 0:1], None, ALU.mult),
                             reads=[skey, "cw"], writes=[("acc", ch)])
                    elif eng == "dve":
                        S.op(eng, lambda e: e.scalar_tensor_tensor(dst, src, cw[:, ch, i:i + 1], dst,
                                                                   ALU.mult, ALU.add),
                             reads=[skey, "cw", ("acc", ch)], writes=[("acc", ch)])
                    else:
                        tdst = ctmp[:].rearrange("p (u r) -> p u r", u=16) if smp else ctmp[:]
                        S.op(eng, lambda e: e.tensor_scalar(tdst, src, cw[:, ch, i:i + 1], None, ALU.mult),
                             reads=[skey, "cw"], writes=["ctmp"])
                        S.op(eng, lambda e: e.tensor_tensor(dst, dst, tdst, ALU.add),
                             reads=["ctmp", ("acc", ch)], writes=[("acc", ch)])
            if smp:
                S.op("act", lambda e: e.activation(cT[:, chs[0]:12, :], acc[:, chs[0]:12, :], AF.Silu),
                     reads=[("acc", ch) for ch in chs], writes=[N("cT")])
            if own:
                S.op("act", lambda e: e.activation(zs[:], B[1][:], AF.Silu), reads=bq(1), writes=[N("zs")])
            yield
            lo = 0 if own else 4
            S.op("act", lambda e: e.activation(sqc[:, lo:8, :], cT[:, lo:8, :], AF.Square),
                 reads=[N("cT")], writes=["sqc"])
            for ch in range(lo, 8):
                bi = 2 + ch // 4
                S.op("pe", lambda e: e.matmul(B[bi][:, (ch % 4) * 128:(ch % 4 + 1) * 128], onesb[:], sqc[:, ch, :],
                                              start=True, stop=True),
                     reads=["sqc", "onesb"], writes=bq(bi, ch % 4))
            for gi in range(lo // 4, 2):
                rv = rst[:, 4 * gi:4 * gi + 4, :].rearrange("p c n -> p (c n)")
                S.op("act", lambda e: e.activation(rv, B[2 + gi][:], AF.Ln, bias=EPS, scale=1.0),
                     reads=bq(2 + gi), writes=[("rst", gi)])
                S.op("act", lambda e: e.activation(rv, rv, AF.Exp, scale=-0.5),
                     reads=[("rst", gi)], writes=[("rst", gi)])
            if own:
                S.op("dve", lambda e: e.scalar_tensor_tensor(cT[:, 0:4, :], rst[:, 0:4, :], 128.0 ** -0.5,
                                                             cT[:, 0:4, :], ALU.mult, ALU.mult),
                     reads=[("rst", 0), N("cT")], writes=[N("cT")])
            S.op("dve", lambda e: e.tensor_tensor(cT[:, 4:8, :], cT[:, 4:8, :], rst[:, 4:8, :], ALU.mult),
                 reads=[("rst", 1), N("cT")], writes=[N("cT")])
            yield
            for h in range(4):
                S.op("pe", lambda e: e.matmul(B[2][:, h * 128:(h + 1) * 128], cT[:, 4 + h, :], identb[:],
                                              start=True, stop=True), reads=[N("cT"), "ident"], writes=bq(2, h))
                S.op("pe", lambda e: e.matmul(B[3][:, h * 128:(h + 1) * 128], cT[:, 8 + h, :], identb[:],
                                              start=True, stop=True), reads=[N("cT"), "ident"], writes=bq(3, h))
            S.op("act", lambda e: e.copy(sm[:, BAS0:BAS1 + 1, :].rearrange("p a b -> p (a b)"), B[0][:, 0:8]),
                 reads=bq(0, 0), writes=[N("bas")])
            S.op("act", lambda e: e.activation(sc(EB), sc(BAS0), AF.Exp, scale=-1.0), reads=[N("bas")], writes=[N("eb")])
            S.op("dve", lambda e: e.tensor_scalar(sc(EB), sc(EB), 1.0, None, ALU.add), reads=[N("eb")], writes=[N("eb")])
            S.op("dve", lambda e: e.reciprocal(sc(BETA), sc(EB)), reads=[N("eb")], writes=[N("beta")])
            S.op("act", lambda e: e.activation(sc(LNEB), sc(EB), AF.Ln), reads=[N("eb")], writes=[N("lneb")])
            S.op("dve", lambda e: e.tensor_tensor(sc(AD), sc(BAS1), dtb[:], ALU.add), reads=[N("bas"), "dtb"], writes=[N("ad")])
            S.op("act", lambda e: e.activation(sc(AD), sc(AD), AF.Exp), reads=[N("ad")], writes=[N("ad")])
            S.op("act", lambda e: e.activation(sc(AD), sc(AD), AF.Ln, bias=1.0), reads=[N("ad")], writes=[N("ad")])
            S.op("dve", lambda e: e.tensor_tensor(sc(AD), sc(AD), nea[:], ALU.mult), reads=[N("ad"), "nea"], writes=[N("ad")])
            S.op("pe", lambda e: e.matmul(B[0][:, 8:12], tri, sc(AD), start=True, stop=True),
                 reads=[N("ad"), "cm"], writes=bq(0, 0))
            S.op("pe", lambda e: e.matmul(B[0][:, 12:16], glones, sc(AD), start=True, stop=True),
                 reads=[N("ad"), "cm", "onesf"], writes=bq(0, 0))
            S.op("act", lambda e: e.copy(sm[:, G_:GL_ + 1, :].rearrange("p a b -> p (a b)"), B[0][:, 8:16]),
                 reads=bq(0, 0), writes=[N("G")])
            S.op("act", lambda e: e.activation(sc(EG), sc(G_), AF.Exp), reads=[N("G")], writes=[N("eG")])
            S.op("dve", lambda e: e.tensor_tensor(sc(EGLG), sc(GL_), sc(G_), ALU.subtract), reads=[N("G")], writes=[N("eglG")])
            S.op("act", lambda e: e.activation(sc(EGLG), sc(EGLG), AF.Exp), reads=[N("eglG")], writes=[N("eglG")])
            S.op("act", lambda e: e.activation(sc(EGL), sc(GL_), AF.Exp), reads=[N("G")], writes=[N("egl")])
            S.op("dve", lambda e: e.scalar_tensor_tensor(sc(NBG), sc(BETA), -1.0, sc(EG), ALU.mult, ALU.mult),
                 reads=[N("beta"), N("eG")], writes=[N("nbG")])
            S.op("dve", lambda e: e.tensor_scalar(sc(NEGG), sc(G_), -1.0, None, ALU.mult), reads=[N("G")], writes=[N("negG")])
            S.op("dve", lambda e: e.tensor_tensor(sc(GLN), sc(G_), sc(LNEB), ALU.subtract),
                 reads=[N("G"), N("lneb")], writes=[N("GLN")])
            for h in range(4):
                S.op("act", lambda e: e.activation(kd[:, h, :], B[2][:, h * 128:(h + 1) * 128], AF.Copy,
                                                   scale=sc(EGLG, h)), reads=bq(2, h) + [N("eglG")], writes=[N(("kd", h))])
                S.op("dve", lambda e: e.tensor_scalar(vb[:, h, :], B[3][:, h * 128:(h + 1) * 128], sc(BETA, h), None,
                                                      ALU.mult), reads=bq(3, h) + [N("beta")], writes=[N(("vb", h))])
            yield
            S.op("dve", lambda e: e.tensor_tensor(diagG[:], identf.unsqueeze(1).to_broadcast([128, 4, 128]),
                                                  sc(G_).unsqueeze(2).to_broadcast([128, 4, 128]), ALU.mult),
                 reads=[N("G"), "cm"], writes=["diagG"])
            S.op("pool", lambda e: e.tensor_tensor(diagL[:], identf.unsqueeze(1).to_broadcast([128, 4, 128]),
                                                   sc(GLN).unsqueeze(2).to_broadcast([128, 4, 128]), ALU.mult),
                 reads=[N("GLN"), "cm"], writes=["diagL"])
            for h in range(4):
                S.op("pe", lambda e: e.matmul(B[0][:, h * 128:(h + 1) * 128], onesf[:], diagG[:, h, :],
                                              start=True, stop=True), reads=["diagG", "onesf"], writes=bq(0, h))
                S.op("pe", lambda e: e.matmul(B[1][:, h * 128:(h + 1) * 128], onesf[:], diagL[:, h, :],
                                              start=True, stop=True), reads=["diagL", "onesf"], writes=bq(1, h))
            b2v = B[0][:].rearrange("p (h n) -> p h n", h=4)
            b3v = B[1][:].rearrange("p (h n) -> p h n", h=4)
            if own:
                S.op("dve", lambda e: e.tensor_tensor(tU[:], b2v, nm_ui.unsqueeze(1).to_broadcast([128, 4, 128]), ALU.add),
                     reads=bq(0) + ["cm"], writes=["tU"])
            S.op("dve", lambda e: e.tensor_tensor(tS[:], b3v, nm_us.unsqueeze(1).to_broadcast([128, 4, 128]), ALU.add),
                 reads=bq(1) + ["cm"], writes=["tS"])
            S.op("dve", lambda e: e.scalar_tensor_tensor(tL[:], b2v, -1.0, nm_ls.unsqueeze(1).to_broadcast([128, 4, 128]),
                                                         ALU.mult, ALU.add), reads=bq(0) + ["cm"], writes=["tL"])
            yield
            for h in range(4):
                if own:
                    S.op("act", lambda e: e.activation(E1[:, h, :], tU[:, h, :], AF.Exp, bias=sc(NEGG, h)),
                         reads=["tU", N("negG")], writes=[("E1", h), "tU"])
                S.op("act", lambda e: e.activation(E2[:, h, :], tS[:, h, :], AF.Exp, bias=sc(NEGG, h)),
                     reads=["tS", N("negG")], writes=[("E2", h), "tS"])
                S.op("act", lambda e: e.activation(E3[:, h, :], tL[:, h, :], AF.Exp, bias=sc(GLN, h)),
                     reads=["tL", N("GLN")], writes=[("E3", h), "tL"])
            for h in range(4):
                S.op("pe", lambda e: e.matmul(B[2][:, h * 128:(h + 1) * 128], cT[:, 4 + h, :], cT[:, 4 + h, :],
                                              start=True, stop=True), reads=[N("cT")], writes=bq(2, h))
                if own:
                    S.op("pe", lambda e: e.matmul(B[3][:, h * 128:(h + 1) * 128], cT[:, 4 + h, :], cT[:, h, :],
                                                  start=True, stop=True), reads=[N("cT")], writes=bq(3, h))
            b4v = B[2][:].rearrange("p (h n) -> p h n", h=4)
            b5v = B[3][:].rearrange("p (h n) -> p h n", h=4)
            E3k = [("E3", h) for h in range(4)]
            E2k = [("E2", h) for h in range(4)]
            E1k = [("E1", h) for h in range(4)]
            S.op("dve", lambda e: e.scalar_tensor_tensor(Mm[0][:], b4v, -1.0, E3[:], ALU.mult, ALU.mult),
                 reads=bq(2) + E3k, writes=[("Mm", 0)])
            S.op("dve", lambda e: e.scalar_tensor_tensor(MT[2][:], b4v, -1.0, E2[:], ALU.mult, ALU.mult),
                 reads=bq(2) + E2k, writes=[N(("MT", 2))])
            if own:
                S.op("dve", lambda e: e.tensor_tensor(qkT[:], b5v, E1[:], ALU.mult), reads=bq(3) + E1k, writes=[N("qkT")])
            S.op("pool", lambda e: e.tensor_tensor(PT[0][:], MT[2][:], identb[:].unsqueeze(1).to_broadcast([128, 4, 128]),
                                                   ALU.add), reads=[N(("MT", 2)), "ident"], writes=[N(("PT", 0))])
            yield
            mcur, tcur = 0, 2
            pcur = 0
            for k in range(1, nlev):
                mnxt = 1 - mcur
                tnxt = 0 if tcur != 0 else 1
                last = (k == nlev - 1)
                for h in range(4):
                    hs = slice(h * 128, (h + 1) * 128)
                    S.op("pe", lambda e: e.matmul(B[0][:, hs], MT[tcur][:, h, :], Mm[mcur][:, h, :], start=True, stop=True),
                         reads=[MK_(tcur), ("Mm", mcur)], writes=bq(0, h))
                    if not last:
                        S.op("pe", lambda e: e.matmul(B[1][:, hs], Mm[mcur][:, h, :], MT[tcur][:, h, :], start=True, stop=True),
                             reads=[MK_(tcur), ("Mm", mcur)], writes=bq(1, h))
                S.op("act", lambda e: e.copy(Mm[mnxt][:].rearrange("p h n -> p (h n)"), B[0][:]),
                     reads=bq(0), writes=[("Mm", mnxt)])
                if not last:
                    S.op("dve", lambda e: e.tensor_copy(MT[tnxt][:].rearrange("p h n -> p (h n)"), B[1][:]),
                         reads=bq(1), writes=[MK_(tnxt)])
                pn = 1 - pcur
                for h in range(4):
                    hs = slice(h * 128, (h + 1) * 128)
                    S.op("pe", lambda e: e.matmul(B[2][:, hs], identb[:], PT[pcur][:, h, :], start=True, stop=False),
                         reads=[N(("PT", pcur)), "ident"], writes=bq(2, h))
                    S.op("pe", lambda e: e.matmul(B[2][:, hs], Mm[mnxt][:, h, :], PT[pcur][:, h, :], start=False, stop=True),
                         reads=[N(("PT", pcur)), ("Mm", mnxt)], writes=bq(2, h))
                S.op("act" if k % 2 else "dve",
                     (lambda e: e.copy(PT[pn][:].rearrange("p h n -> p (h n)"), B[2][:])) if k % 2 else
                     (lambda e: e.tensor_copy(PT[pn][:].rearrange("p h n -> p (h n)"), B[2][:])),
                     reads=bq(2), writes=[N(("PT", pn))])
                yield
                mcur = mnxt
                tcur = tnxt
                pcur = pn
            Ainv = PT[pcur]
            akey = N(("PT", pcur))
            yield "SCAN"
            KS, QS, VS, OS = slice(0, 128), slice(128, 256), slice(256, 384), slice(384, 512)
            if smp:
                for h in range(4):
                    for u in range(16):
                        us = slice(h * 128 + 8 * u, h * 128 + 8 * u + 8)
                        S.op("pe", lambda e: e.matmul(B[2][:, us], Ssmbb[h % 2][:, u, :], cT[:, 4 + h, 8 * u:8 * u + 8],
                                                      start=True, stop=True), reads=[N("cT"), ("Ssmb", h % 2)], writes=bq(2, h))
                        S.op("pe", lambda e: e.matmul(B[3][:, us], Ssmbb[h % 2][:, u, :], cT[:, h, 8 * u:8 * u + 8],
                                                      start=True, stop=True), reads=[N("cT"), ("Ssmb", h % 2)], writes=bq(3, h))
                    if h + 2 < 4:
                        S.dma("pool", Ssmbb[h % 2][:], io["sdelta"][:, h + 2].rearrange("u k v -> k u v"),
                              writes=[("Ssmb", h % 2)])
                S.op("act", lambda e: e.copy(kSTs[:, 0:4, :].rearrange("p h n -> p (h n)"), B[2][:]),
                     reads=bq(2), writes=["kSTs0"])
                S.op("dve", lambda e: e.tensor_copy(kSTs[:, 4:8, :].rearrange("p h n -> p (h n)"), B[3][:]),
                     reads=bq(3), writes=["kSTs1"])
                S.op("dve", lambda e: e.tensor_tensor(gsel[:], cm[:, 10, 0:16].unsqueeze(1).to_broadcast([128, 4, 16]),
                                                      sc(AD).unsqueeze(2).to_broadcast([128, 4, 16]), ALU.mult),
                     reads=[N("ad"), "cm"], writes=["gsel"])
                for h in range(4):
                    S.op("pe", lambda e: e.matmul(B[2][:, h * 16:(h + 1) * 16], onesf[:], gsel[:, h, :],
                                                  start=True, stop=True), reads=["gsel", "onesf"], writes=bq(2, 0))
                S.op("act", lambda e: e.activation(eglb[:].rearrange("p h u -> p (h u)"), B[2][:, 0:64], AF.Exp),
                     reads=bq(2, 0), writes=["eglb"])
            HS = [slice(h * 128, (h + 1) * 128) for h in range(4)]
            for h in range(4):
                Bh = B[4 + h]
                if not smp:
                    S.op("pe", lambda e: e.matmul(Bh[:, KS], cT[:, 4 + h, :], Sbf[:, h, :], start=True, stop=True),
                         reads=[N("cT"), ("Sbf", h)], writes=bq(4 + h, 0))
                    if own:
                        S.op("pe", lambda e: e.matmul(Bh[:, QS], cT[:, h, :], Sbf[:, h, :], start=True, stop=True),
                             reads=[N("cT"), ("Sbf", h)], writes=bq(4 + h, 1))
                else:
                    S.op("pe", lambda e: e.matmul(Bh[:, KS], kSTs[:, h, :], identf, start=True, stop=True),
                         reads=["kSTs0", "cm"], writes=bq(4 + h, 0))
                    S.op("pe", lambda e: e.matmul(Bh[:, QS], kSTs[:, 4 + h, :], identf, start=True, stop=True),
                         reads=["kSTs1", "cm"], writes=bq(4 + h, 1))
            for h in range(4):
                Bh = B[4 + h]
                S.op("dve", lambda e: e.scalar_tensor_tensor(R32[:, h, :], Bh[:, KS], sc(NBG, h), vb[:, h, :],
                                                             ALU.mult, ALU.add),
                     reads=bq(4 + h, 0) + [N("nbG"), N(("vb", h))], writes=[("R32", h)])
                S.op("act", lambda e: e.copy(Rr[:, h, :], R32[:, h, :]), reads=[("R32", h)], writes=[("Rr", h)])
                if own:
                    S.op("act", lambda e: e.activation(tmpo[:, h, :], Bh[:, QS], AF.Copy, scale=sc(EG, h)),
                         reads=bq(4 + h, 1) + [N("eG")], writes=[("tmpo", h)])
            yield
            for h in range(4):
                Bh = B[4 + h]
                S.op("pe", lambda e: e.matmul(Bh[:, VS], Ainv[:, h, :], Rr[:, h, :], start=True, stop=True),
                     reads=[akey, ("Rr", h)], writes=bq(4 + h, 2))
            for h in range(4):
                Bh = B[4 + h]
                S.op("act", lambda e: e.copy(v1f[:, h, :], Bh[:, VS]), reads=bq(4 + h, 2), writes=[("v1f", h)])
                S.op("dve", lambda e: e.tensor_copy(v1b[:, h, :], v1f[:, h, :]), reads=[("v1f", h)], writes=[("v1b", h)])
                S.op("dve", lambda e: e.tensor_tensor(t1[:, h, :], R32[:, h, :], v1f[:, h, :], ALU.subtract),
                     reads=[("v1f", h), ("R32", h)], writes=[("t1", h)])
            yield
            for h in range(4):
                Bh = B[4 + h]
                S.op("pe", lambda e: e.matmul(Bh[:, KS], MT[2][:, h, :], v1b[:, h, :], start=True, stop=True),
                     reads=[N(("MT", 2)), ("v1b", h)], writes=bq(4 + h, 0))
            for h in range(4):
                Bh = B[4 + h]
                S.op("dve", lambda e: e.tensor_tensor(r1b[:, h, :], t1[:, h, :], Bh[:, KS], ALU.add),
                     reads=bq(4 + h, 0) + [("t1", h)], writes=[("r1b", h)])
            yield
            for h in range(4):
                Bh = B[4 + h]
                S.op("pe", lambda e: e.matmul(Bh[:, QS], Ainv[:, h, :], r1b[:, h, :], start=True, stop=True),
                     reads=[akey, ("r1b", h)], writes=bq(4 + h, 1))
            for h in range(4):
                Bh = B[4 + h]
                S.op("dve", lambda e: e.tensor_tensor(vnew[:, h, :], v1f[:, h, :], Bh[:, QS], ALU.add),
                     reads=bq(4 + h, 1) + [("v1f", h)], writes=[("vnew", h)])
            yield
            for h in range(4):
                Bh = B[4 + h]
                if own:
                    S.op("pe", lambda e: e.matmul(Bh[:, VS], qkT[:, h, :], vnew[:, h, :], start=True, stop=True),
                         reads=[N("qkT"), ("vnew", h)], writes=bq(4 + h, 2))
                if not smp:
                    S.op("pe", lambda e: e.matmul(Bh[:, OS], kd[:, h, :], vnew[:, h, :], start=True, stop=True),
                         reads=[N(("kd", h)), ("vnew", h)], writes=bq(4 + h, 3))
            for h in range(4):
                Bh = B[4 + h]
                if own:
                    S.op("dve", lambda e: e.tensor_tensor(o32[:, h, :], tmpo[:, h, :], Bh[:, VS], ALU.add),
                         reads=bq(4 + h, 2) + [("tmpo", h)], writes=[("o32", h)])
                if not smp:
                    S.op("dve", lambda e: e.scalar_tensor_tensor(S32[:, h, :], S32[:, h, :], sc(EGL, h), Bh[:, OS],
                                                                 ALU.mult, ALU.add),
                         reads=bq(4 + h, 3) + [N("egl"), ("S32", h)], writes=[("S32", h)])
                    S.op("act", lambda e: e.copy(Sbf[:, h, :], S32[:, h, :]), reads=[("S32", h)], writes=[("Sbf", h)])
            yield
            if smp:
                for h in range(4):
                    S.op("pool", lambda e: e.tensor_tensor(
                        vblk, vnew[:, h, :].unsqueeze(1).to_broadcast([128, 16, 128]),
                        cm[:, 10, 0:16].unsqueeze(2).to_broadcast([128, 16, 128]), ALU.mult),
                        reads=[("vnew", h), "cm"], writes=ACCK)
                    for q4 in range(4):
                        bi = (0, 1, 2, 3)[q4]
                        S.op("pe", lambda e: e.matmul(B[bi][:], kd[:, h, :],
                                                      vblk[:, 4 * q4:4 * q4 + 4, :].rearrange("p u v -> p (u v)"),
                                                      start=True, stop=True),
                             reads=[N(("kd", h))] + ACCK, writes=bq(bi))
                        sl = Ssm32b[h % 2][:, 4 * q4:4 * q4 + 4, :]
                        S.op("pool", lambda e: e.tensor_tensor(
                            sl, sl, eglb[:, h, 4 * q4:4 * q4 + 4].unsqueeze(2).to_broadcast([128, 4, 128]), ALU.mult),
                            reads=["eglb", ("Ssm32", h % 2)], writes=[("Ssm32", h % 2)])
                        S.op("dve", lambda e: e.tensor_tensor(sl, sl, B[bi][:].rearrange("p (u v) -> p u v", u=4), ALU.add),
                             reads=bq(bi) + [("Ssm32", h % 2)], writes=[("Ssm32", h % 2)])
                    S.dma("sp", io["ds_o"][:, h].rearrange("u k v -> k u v"), Ssm32b[h % 2][:], reads=[("Ssm32", h % 2)])
                    if h + 2 < 4:
                        S.dma("sp", Ssm32b[h % 2][:], io["sdelta"][:, h + 2].rearrange("u k v -> k u v"),
                              writes=[("Ssm32", h % 2)])
            if own:
                okeys = [("o32", h) for h in range(4)]
                S.op("act", lambda e: e.activation(sqo, o32[:].rearrange("p h n -> p (h n)"), AF.Square),
                     reads=okeys, writes=T1K)
                S.op("dve", lambda e: e.tensor_reduce(sc(SSO), sqo.rearrange("p (h d) -> p h d", d=128), AX.X, ALU.add),
                     reads=T1K, writes=["sso"])
                S.op("act", lambda e: e.activation(sc(RSO), sc(SSO), AF.Sqrt, bias=EPS, scale=1.0 / 128),
                     reads=["sso"], writes=["rso"])
                S.op("dve", lambda e: e.reciprocal(sc(RSO), sc(RSO)), reads=["rso"], writes=["rso"])
                S.op("dve", lambda e: e.tensor_tensor(o32[:], o32[:], sc(RSO).unsqueeze(2).to_broadcast([128, 4, 128]), ALU.mult),
                     reads=okeys + ["rso"], writes=okeys)
                S.op("pool", lambda e: e.tensor_tensor(o32[:], o32[:], dng[:].unsqueeze(1).to_broadcast([128, 4, 128]), ALU.mult),
                     reads=okeys + ["dng"], writes=okeys)
                S.op("pool", lambda e: e.tensor_tensor(obf[:], o32[:].rearrange("p h n -> p (h n)"), zs[:], ALU.mult),
                     reads=okeys + [N("zs")], writes=["obf"])
                for h in range(4):
                    S.op("pe", lambda e: e.matmul(B[3][:, h * 128:(h + 1) * 128], obf[:, h * 128:(h + 1) * 128], identb[:],
                                                  start=True, stop=True), reads=["obf", "ident"], writes=bq(3, h))
                S.op("act", lambda e: e.copy(self.mixT[:, 0:4, orow:orow + 128], B[3][:].rearrange("p (h n) -> p h n", h=4)),
                     reads=bq(3), writes=[("mixT", t)])
            if t == 31:
                S.dma("sp", io["dp_o"].rearrange("h k v -> k h v"), S32[:], reads=[("S32", h) for h in range(4)])
                for ch in range(12):
                    bi = 2 + ch // 4
                    S.op("pe", lambda e: e.matmul(B[bi][0:3, (ch % 4) * 128:(ch % 4 + 1) * 128], ub[:, ch, 128:131],
                                                  identf, start=True, stop=True), reads=[("uT", b), "cm"], writes=bq(bi, ch % 4))
                for gi in range(3):
                    S.op("act", lambda e: e.copy(cps[0:3, gi * 512:(gi + 1) * 512], B[2 + gi][0:3, :]),
                         reads=bq(2 + gi), writes=ACCK)
                S.dma("sp", io["cp_o"], cps[0:3, :], reads=ACCK)
            if smp:
                S.op("dve", lambda e: e.tensor_copy(cs3[:].rearrange("p c (u r) -> p c u r", u=16), uS[:, :, :, 8:11]),
                     reads=["uS"], writes=["cs3"])
                for ch in range(12):
                    bi = 2 + ch // 4
                    S.op("pe", lambda e: e.matmul(B[bi][0:48, (ch % 4) * 128:(ch % 4 + 1) * 128], cs3[:, ch, :],
                                                  identf, start=True, stop=True), reads=["cs3", "cm"], writes=bq(bi, ch % 4))
                for gi in range(3):
                    S.op("act", lambda e: e.copy(cps[:, gi * 512:(gi + 1) * 512], B[2 + gi][0:48, :]),
                         reads=bq(2 + gi), writes=ACCK)
                S.dma("sp", io["cs_o"].rearrange("u r c -> (u r) c"), cps, reads=ACCK)

        alloc_par(1)
        Dg = TP("Dg", [128, 48, 128], BF16)
        ubf = TP("ubf", [128, 12, 131], BF16)
        S.op("dve", lambda e: e.tensor_tensor(Dg[:], identb[:].unsqueeze(1).to_broadcast([128, 48, 128]),
                                              cw[:].rearrange("p c i -> p (c i)").unsqueeze(2).to_broadcast([128, 48, 128]),
                                              ALU.mult), reads=["ident", "cw"], writes=["Dg"])
        self.load_x(0, 0)
        self.load_x(1, 1)
        self.norm_T(0, self.g1, "g1", lnexp=True)
        def drive(gen_scan, gen_pre):
            pre_done = gen_pre is None
            scan_done = gen_scan is None
            while not (pre_done and scan_done):
                if not scan_done:
                    try:
                        next(gen_scan)
                    except StopIteration:
                        scan_done = True
                if not pre_done:
                    for _ in range(2):
                        if next(gen_pre) == "SCAN":
                            pre_done = True
                            break

        gens = [tileA(t) for t in range(32)]
        drive(None, gens[0])
        for t in range(32):
            drive(gens[t], gens[t + 1] if t + 1 < 32 else None)
        S.barrier()
        stP.close()
        uS = T2("uS", [128, 12, 16, 11], F32)
        cs3 = T2("cs3", [128, 12, 48], F32)
        Ssm32b = [T2("Ssm32_%d" % i, [128, 16, 128], F32) for i in range(2)]
        Ssmbb = [T2("Ssmb_%d" % i, [128, 16, 128], BF16) for i in range(2)]
        kSTs = T2("kSTs", [128, 8, 128], F32)
        for _ in tileA(32):
            pass


def run_interleaved(gen_fn, items, depth=2):
    active = []
    nxt = 0
    while nxt < len(items) or active:
        while nxt < len(items) and len(active) < depth:
            active.append(gen_fn(items[nxt]))
            nxt += 1
            break
        for g in list(active):
            try:
                next(g)
            except StopIteration:
                active.remove(g)


_PROG_CACHE = {}


def get_prog(phases):
    key = tuple(phases)
    if key not in _PROG_CACHE:
        p = Prog(phases)
        p.build()
        _PROG_CACHE[key] = p
    return _PROG_CACHE[key]


def make_core_inputs(c, I):
    s, half = c // 2, c % 2
    f = np.float32
    xp = np.asarray(I["x_prompt"], f)
    xs = np.asarray(I["x_sample"], f)
    xl = np.zeros((NLOC, D), f)
    if half == 1:
        xl[0:NPRE] = xp[s, 0:2048]
    xl[NPRE:NPRE + NOWN] = xp[s, half * 2048:(half + 1) * 2048]
    xl[NPRE + NOWN:] = xs[16 * c:16 * c + 16].reshape(128, D)
    w_in = np.asarray(I["w_in"], f)[0]
    m = {
        "xl": xl,
        "w_inA": np.ascontiguousarray(w_in[:, 0:2056]),
        "w_inB": np.ascontiguousarray(w_in[:, 2056:3592]),
        "g1bc": np.ascontiguousarray(np.broadcast_to(np.asarray(I["norm1_g"], f)[0][None, :], (128, D))),
        "qgbc": np.ascontiguousarray(np.broadcast_to(np.asarray(I["q_norm_g"], f)[0][None, :], (128, 64))),
        "kgbc": np.ascontiguousarray(np.broadcast_to(np.asarray(I["k_norm_g"], f)[0][None, :], (128, 64))),
    }
    cw = np.asarray(I["conv_w"], f)[0]
    m["cwT"] = np.ascontiguousarray(cw.T.reshape(12, 128, 4).transpose(1, 0, 2))
    m["alogbc"] = np.ascontiguousarray(np.broadcast_to(np.asarray(I["a_log"], f)[0][None, :], (128, 4)))
    m["dtbbc"] = np.ascontiguousarray(np.broadcast_to(np.asarray(I["dt_bias"], f)[0][None, :], (128, 4)))
    m["dngbc"] = np.ascontiguousarray(np.broadcast_to(np.asarray(I["delta_norm_g"], f)[0][None, :], (128, 128)))
    m["sdelta"] = np.ascontiguousarray(np.asarray(I["state_delta"], f)[0, 16 * c:16 * c + 16])
    m["sconv"] = np.ascontiguousarray(np.asarray(I["state_conv"], f)[0, 16 * c:16 * c + 16])
    m["pvalid"] = np.full((128, 64), float(half), f)
    m["w_o"] = np.asarray(I["w_o"], f)[0]
    m["w_up"] = np.asarray(I["w_up"], f)[0]
    m["w_down"] = np.asarray(I["w_down"], f)[0]
    m["g2bc"] = np.ascontiguousarray(np.broadcast_to(np.asarray(I["norm2_g"], f)[0][None, :], (128, D)))
    m["ck"] = np.asarray(I["cache_swa_k"], f)[0, 16 * c:16 * c + 16].reshape(16, 2048, 512)
    m["cv"] = np.asarray(I["cache_swa_v"], f)[0, 16 * c:16 * c + 16].reshape(16, 2048, 512)
    m.update(host_consts())
    return m


def kernel(_phases=("B", "ATT", "A", "MLP"), **I):
    prog = get_prog(_phases)
    in_maps = []
    for c in range(NCORES):
        m = make_core_inputs(c, I)
        in_maps.append({k: m[k] for k in prog.ins})
    res = run_bass_kernel_spmd(prog.nc, in_maps, core_ids=list(range(NCORES)))
    R = res.results
    f = np.float32
    y_p = np.zeros((4, 4096, D), f)
    y_s = np.zeros((128, 8, D), f)
    k_p = np.zeros((1, 4, 2048, 8, 64), f)
    v_p = np.zeros((1, 4, 2048, 8, 64), f)
    d_p = np.zeros((1, 4, 4, 128, 128), f)
    c_p = np.zeros((1, 4, 3, 1536), f)
    k_s = np.zeros((1, 128, 8, 8, 64), f)
    v_s = np.zeros((1, 128, 8, 8, 64), f)
    d_s = np.zeros((1, 128, 4, 128, 128), f)
    c_s = np.zeros((1, 128, 3, 1536), f)
    for c in range(NCORES):
        s, half = c // 2, c % 2
        r = R[c]
        y_p[s, half * 2048:(half + 1) * 2048] = r["y_o"][0:NOWN]
        y_s[16 * c:16 * c + 16] = r["y_o"][NOWN:].reshape(16, 8, D)
        if half == 1:
            k_p[0, s] = r["k_o"][0:NOWN].reshape(2048, 8, 64)
            v_p[0, s] = r["v_o"][0:NOWN].reshape(2048, 8, 64)
            if "dp_o" in r:
                d_p[0, s] = r["dp_o"]
                c_p[0, s] = r["cp_o"]
        k_s[0, 16 * c:16 * c + 16] = r["k_o"][NOWN:].reshape(16, 8, 8, 64)
        v_s[0, 16 * c:16 * c + 16] = r["v_o"][NOWN:].reshape(16, 8, 8, 64)
        if "ds_o" in r:
            d_s[0, 16 * c:16 * c + 16] = r["ds_o"]
            c_s[0, 16 * c:16 * c + 16] = r["cs_o"]
    return (y_p, y_s, k_p, v_p, d_p, c_p, k_s, v_s, d_s, c_s)
```

```python
import contextlib
import numpy as np
import ml_dtypes
import concourse.bass as bass
import concourse.mybir as mybir
from concourse.bass_utils import run_bass_kernel_spmd

F32 = mybir.dt.float32
BF16 = mybir.dt.bfloat16
AF = mybir.ActivationFunctionType
ALU = mybir.AluOpType
AX = mybir.AxisListType

NCORES = 8
D = 1024
NPRE = 2048
NOWN = 2048
NSMP = 128
NLOC = NPRE + NOWN + NSMP
NOUT = NOWN + NSMP
EPS = 1e-6


class Sched:
    def __init__(self, nc, stack, n_dma_sems=48):
        self.nc = nc
        self.engs = {"pe": nc.tensor, "act": nc.scalar, "dve": nc.vector,
                     "pool": nc.gpsimd, "sp": nc.sync}
        self.sem = {k: stack.enter_context(nc.semaphore("s_" + k)) for k in self.engs}
        self.cnt = {k: 0 for k in self.engs}
        self.known = {k: {} for k in self.engs}
        self.snap = {}
        self.last_w = {}
        self.reads = {}
        self.dsem = [stack.enter_context(nc.semaphore("d%d" % i)) for i in range(n_dma_sems)]
        self.dcnt = [0] * n_dma_sems
        self.drr = 0
        self.ninst = 0

    def _semof(self, key):
        if isinstance(key, tuple):
            return self.dsem[key[1]]
        return self.sem[key]

    def _wait(self, e, deps):
        kn = self.known[e]
        for (k, v) in sorted(deps, key=lambda t: str(t)):
            if kn.get(k, 0) >= v:
                continue
            if k == "pe" and e == "pe":
                continue
            self.engs[e].wait_ge(self._semof(k), v)
            sn = self.snap.get((k, v))
            if sn:
                for k2, v2 in sn.items():
                    if kn.get(k2, 0) < v2:
                        kn[k2] = v2
            kn[k] = v

    @staticmethod
    def _expand(keys):
        out = []
        for k in keys:
            if isinstance(k, tuple) and len(k) == 2 and k[0] == "bank":
                out.extend(("b", k[1], q) for q in range(4))
            else:
                out.append(k)
        return out

    def _deps(self, reads, writes):
        deps = set()
        for r in reads:
            lw = self.last_w.get(r)
            if lw:
                deps.add(lw)
        for w in writes:
            lw = self.last_w.get(w)
            if lw:
                deps.add(lw)
            for rd in self.reads.get(w, ()):
                deps.add(rd)
        return deps

    def _record(self, src, reads, writes):
        for r in reads:
            self.reads.setdefault(r, []).append(src)
        for w in writes:
            self.last_w[w] = src
            self.reads[w] = []

    @staticmethod
    def _bankx(reads, writes):
        banks = {k[1] for k in list(reads) + list(writes) if isinstance(k, tuple) and len(k) == 3 and k[0] == "b"}
        return list(writes) + [("bx", i) for i in sorted(banks)]

    def op(self, e, fn, reads=(), writes=()):
        reads, writes = self._expand(reads), self._expand(writes)
        writes = self._bankx(reads, writes)
        self._wait(e, self._deps(reads, writes))
        inst = fn(self.engs[e])
        self.cnt[e] += 1
        n = self.cnt[e]
        inst.then_inc(self.sem[e], 1)
        self.ninst += 1
        sn = dict(self.known[e])
        sn[e] = n
        self.snap[(e, n)] = sn
        self._record((e, n), reads, writes)
        return inst

    def prewait(self, e, rw_list):
        deps = set()
        for reads, writes in rw_list:
            reads, writes = self._expand(reads), self._expand(writes)
            writes = self._bankx(reads, writes)
            deps |= self._deps(reads, writes)
        self._wait(e, deps)

    def dma(self, q, out, in_, reads=(), writes=(), **kw):
        i = self.drr
        self.drr = (self.drr + 1) % len(self.dsem)
        reads, writes = self._expand(reads), self._expand(writes)
        deps = self._deps(reads, writes)
        if self.dcnt[i]:
            deps.add((("d", i), self.dcnt[i]))
        self._wait(q, deps)
        inst = self.engs[q].dma_start(out=out, in_=in_, **kw)
        self.dcnt[i] += 16
        inst.then_inc(self.dsem[i], 16)
        src = (("d", i), self.dcnt[i])
        self.snap[src] = dict(self.known[q])
        self._record(src, reads, writes)
        self.ninst += 1
        return inst

    def barrier(self):
        deps = set()
        for k in self.engs:
            if self.cnt[k]:
                deps.add((k, self.cnt[k]))
        for i, c in enumerate(self.dcnt):
            if c:
                deps.add((("d", i), c))
        for e in self.engs:
            self._wait(e, deps)

    def finish(self):
        deps = set()
        for k in self.engs:
            if self.cnt[k]:
                deps.add((k, self.cnt[k]))
        for i, c in enumerate(self.dcnt):
            if c:
                deps.add((("d", i), c))
        self._wait("sp", deps)


def mult_m(delta):
    d = np.asarray(delta)
    m = ((d >= 0) & (d <= 128)).astype(np.float32)
    m += ((d >= 0) & (d <= 512) & (d % 4 == 0))
    m += ((d >= 0) & (d <= 2048) & (d % 16 == 0))
    return m.astype(np.float32)


NEG = -30000.0


def host_consts():
    c = {}
    c["ident"] = np.eye(128, dtype=np.float32)
    j = np.arange(128)[:, None]
    i = np.arange(128)[None, :]
    same = (j // 8) == (i // 8)
    cm = np.zeros((128, 11, 128), np.float32)
    cm[:, 0] = (j <= i)
    cm[:, 1] = (j <= i) & same
    cm[:, 2] = same
    cm[:, 3] = np.where(i >= j, 0.0, NEG)
    cm[:, 4] = np.where(i > j, 0.0, NEG)
    cm[:, 5] = np.where(i < j, 0.0, NEG)
    cm[:, 6] = np.where((i >= j) & same, 0.0, NEG)
    cm[:, 7] = np.where((i > j) & same, 0.0, NEG)
    cm[:, 8] = np.where((i < j) & same, 0.0, NEG)
    cm[:, 9] = np.eye(128)
    cm[:, 10, 0:16] = (np.arange(128)[:, None] // 8) == np.arange(16)[None, :]
    c["cmask"] = cm
    k = np.arange(128)[:, None, None, None]
    rel = np.arange(20)[None, :, None, None]
    jj = np.arange(4)[None, None, :, None]
    q = np.arange(128)[None, None, None, :]
    c["mk"] = mult_m((16 + jj - rel) * 128 + q - k).reshape(128, 20, 512)
    p = np.arange(128)[:, None, None]
    kt = np.arange(16)[None, :, None]
    i = (np.arange(64) % 8)[None, None, :]
    c["msmp"] = mult_m(2048 + i - (128 * kt + p)).astype(np.float32)
    u = np.arange(16)[None, :, None]
    c["mnew"] = (mult_m(i - (p % 8)) * ((p // 8) == u)).astype(np.float32)
    return c


class Prog:
    def __init__(self, phases=("B", "ATT", "A", "MLP")):
        self.phases = phases
        self.nc = bass.Bass("TRN2", target_bir_lowering=False)
        self.ins = {}
        self.outs = {}

    def din(self, name, shape, dt=F32):
        ap = self.nc.dram_tensor(name, list(shape), dt, kind="ExternalInput").ap()
        self.ins[name] = ap
        return ap

    def dout(self, name, shape, dt=F32):
        ap = self.nc.dram_tensor(name, list(shape), dt, kind="ExternalOutput").ap()
        self.outs[name] = ap
        return ap

    def build(self):
        nc = self.nc
        xl = self.din("xl", [NLOC, D])
        w_inA = self.din("w_inA", [D, 2056])
        w_inB = self.din("w_inB", [D, 1536])
        g1bc = self.din("g1bc", [128, D])
        qgbc = self.din("qgbc", [128, 64])
        kgbc = self.din("kgbc", [128, 64])
        ident_d = self.din("ident", [128, 128])
        self.io_extra = dict(
            cmask=self.din("cmask", [128, 11, 128]),
            cwT=self.din("cwT", [128, 12, 4]),
            alogbc=self.din("alogbc", [128, 4]),
            dtbbc=self.din("dtbbc", [128, 4]),
            dngbc=self.din("dngbc", [128, 128]),
            sdelta=self.din("sdelta", [16, 4, 128, 128]),
            sconv=self.din("sconv", [16, 3, 1536]),
            dp_o=self.dout("dp_o", [4, 128, 128]),
            cp_o=self.dout("cp_o", [3, 1536]),
            ds_o=self.dout("ds_o", [16, 4, 128, 128]),
            cs_o=self.dout("cs_o", [16, 3, 1536]),
            mk=self.din("mk", [128, 20, 512]),
            msmp=self.din("msmp", [128, 16, 64]),
            mnew=self.din("mnew", [128, 16, 64]),
            pvalid=self.din("pvalid", [128, 64]),
            ck=self.din("ck", [16, 2048, 512]),
            w_o=self.din("w_o", [D, D]),
            w_up=self.din("w_up", [D, 4 * D]),
            w_down=self.din("w_down", [4 * D, D]),
            g2bc=self.din("g2bc", [128, D]),
            cv=self.din("cv", [16, 2048, 512]),
        )
        y_o = self.dout("y_o", [NOUT, D])
        k_o = self.dout("k_o", [NOUT, 512])
        v_o = self.dout("v_o", [NOUT, 512])
        self.io = dict(xl=xl, w_inA=w_inA, w_inB=w_inB, g1bc=g1bc, qgbc=qgbc, kgbc=kgbc,
                       ident=ident_d, y_o=y_o, k_o=k_o, v_o=v_o)
        self.io.update(self.io_extra)
        with contextlib.ExitStack() as st:
            self.st = st
            S = self.S = Sched(nc, st)
            self.T = lambda name, shape, dt: st.enter_context(nc.sbuf_tensor("sb_" + name, list(shape), dt))
            T = self.T
            self.bank = [st.enter_context(nc.psum_tensor("bank%d" % i, [128, 512], F32)) for i in range(8)]
            self.ident = T("ident", [128, 128], BF16)
            S.dma("pool", self.ident[:], ident_d, writes=["ident"])
            self.g1 = T("g1", [128, D], F32)
            S.dma("sp", self.g1[:], g1bc, writes=["g1"])
            self.xt = [T("xt%d" % i, [128, D], F32) for i in range(2)]
            self.junk = T("junk", [128, D], BF16)
            self.ss = [T("ss%d" % i, [128, 1], F32) for i in range(2)]
            self.rs = [T("rs%d" % i, [128, 1], F32) for i in range(2)]
            self.xn = [T("xn%d" % i, [128, D], BF16) for i in range(2)]
            self.xnT = [T("xnT%d" % i, [128, 8, 128], BF16) for i in range(2)]
            self.mixT = T("mixT", [128, 8, NOUT], BF16)
            self.KTs = T("KTs", [128, 4, 128], BF16)
            self.QTs = T("QTs", [128, 4, 128], BF16)
            self.Vbs = T("Vbs", [128, 512], BF16)
            with contextlib.ExitStack() as st1:
                T1 = lambda name, shape, dt: st1.enter_context(nc.sbuf_tensor("sb_" + name, list(shape), dt))
                self.KT = T1("KT", [128, 4, NLOC], BF16)
                self.QT = T1("QT", [128, 4, NOUT], BF16)
                self.Vb = T1("Vb", [128, 33, 512], BF16)
                if "B" in self.phases:
                    with contextlib.ExitStack() as st2:
                        self.pass_B(st2)
                S.barrier()
                if "ATT" in self.phases:
                    with contextlib.ExitStack() as st2:
                        self.attention(st2)
                S.barrier()
            if "ATT" in self.phases:
                with contextlib.ExitStack() as st2:
                    self.attention_smp(st2)
                S.barrier()
            if "A" in self.phases:
                with contextlib.ExitStack() as st2:
                    self.pass_A(st2)
                S.barrier()
            if "MLP" in self.phases:
                with contextlib.ExitStack() as st2:
                    self.mlp(st2)
            S.finish()
        return nc

    def load_x(self, t, b):
        self.S.dma("sp", self.xt[b][:], self.io["xl"][t * 128:(t + 1) * 128, :], writes=[("xt", b)])

    def norm_T(self, b, gbc, gkey, part=0, lnexp=False):
        S = self.S
        xt, ss, rs, xn, xnT = self.xt[b], self.ss[b], self.rs[b], self.xn[b], self.xnT[b]
        if part in (0, 1):
            S.op("act", lambda e: e.activation(self.junk[:], xt[:], AF.Square, accum_out=ss[:]),
                 reads=[("xt", b)], writes=["junk", ("ss", b)])
            if lnexp:
                S.op("act", lambda e: e.activation(rs[:], ss[:], AF.Ln, bias=EPS, scale=1.0 / D),
                     reads=[("ss", b)], writes=[("rs", b)])
                S.op("act", lambda e: e.activation(rs[:], rs[:], AF.Exp, scale=-0.5),
                     reads=[("rs", b)], writes=[("rs", b)])
            else:
                S.op("act", lambda e: e.activation(rs[:], ss[:], AF.Sqrt, bias=EPS, scale=1.0 / D),
                     reads=[("ss", b)], writes=[("rs", b)])
                S.op("dve", lambda e: e.reciprocal(rs[:], rs[:]), reads=[("rs", b)], writes=[("rs", b)])
            S.op("dve", lambda e: e.scalar_tensor_tensor(xn[:], xt[:], rs[:, 0:1], gbc[:], ALU.mult, ALU.mult),
                 reads=[("xt", b), ("rs", b), gkey], writes=[("xn", b)])
        if part == 1:
            return
        for kc in range(8):
            bk = self.bank[kc // 4]
            S.op("pe", lambda e: e.matmul(bk[:, (kc % 4) * 128:(kc % 4 + 1) * 128],
                                          xn[:, kc * 128:(kc + 1) * 128], self.ident[:],
                                          start=True, stop=True),
                 reads=[("xn", b), "ident"], writes=[("bank", kc // 4)])
        S.op("act", lambda e: e.copy(xnT[:, 0:4, :].rearrange("p a b -> p (a b)"), self.bank[0][:]),
             reads=[("bank", 0)], writes=[("xnT", b)])
        S.op("dve", lambda e: e.tensor_copy(xnT[:, 4:8, :].rearrange("p a b -> p (a b)"), self.bank[1][:]),
             reads=[("bank", 1)], writes=[("xnT", b)])

    def pass_B(self, st2):
        nc, S, T = self.nc, self.S, self.T
        T2 = lambda name, shape, dt: st2.enter_context(nc.sbuf_tensor("sb_" + name, list(shape), dt))
        wB = T2("wB", [128, 8, 1536], BF16)
        wsrc = self.io["w_inB"].rearrange("(kc p) n -> p kc n", p=128)
        for kc in range(8):
            S.dma("pool", wB[:, kc, :], wsrc[:, kc, :], writes=[("wB", kc)])
        qg = T2("qg", [128, 64], F32)
        kg = T2("kg", [128, 64], F32)
        S.dma("sp", qg[:], self.io["qgbc"], writes=["qg"])
        S.dma("sp", kg[:], self.io["kgbc"], writes=["kg"])
        sq = T2("sqB", [128, 512], F32)
        ss8 = [T2("ss8_%d" % i, [128, 8], F32) for i in range(2)]
        nrm = [T2("nrm%d" % i, [128, 512], F32) for i in range(4)]
        nbf = [T2("nbf%d" % i, [128, 512], BF16) for i in range(2)]
        v32 = [T2("v32_%d" % i, [128, 512], F32) for i in range(2)]
        wkeys = [("wB", kc) for kc in range(8)]
        self.load_x(0, 0)

        def stage1(t):
            if t + 1 < 33:
                self.load_x(t + 1, (t + 1) % 2)
            self.norm_T(t % 2, self.g1, "g1")

        def tileB(t):
            b = t % 2
            xnT = self.xnT[b]
            own = t >= 16
            groups = [0, 1, 2] if own else [1, 2]
            for g in groups:
                bk = self.bank[2 + g]
                for kc in range(8):
                    S.op("pe", lambda e: e.matmul(bk[:], xnT[:, kc, :], wB[:, kc, g * 512:(g + 1) * 512],
                                                  start=(kc == 0), stop=(kc == 7)),
                         reads=[("xnT", b), ("wB", kc)], writes=[("bank", 2 + g)])
            yield
            orow = (t - 16) * 128
            vb = self.bank[4]
            vv = v32[t % 2]
            if own:
                S.op("act", lambda e: e.copy(vv[:], vb[:]), reads=[("bank", 4)], writes=[("v32", t % 2)])
                S.dma("sp", self.io["v_o"][orow:orow + 128, :], vv[:], reads=[("v32", t % 2)])
            S.op("dve", lambda e: e.tensor_copy(self.Vb[:, t, :], vb[:]), reads=[("bank", 4)], writes=[("Vb", t)])
            for g in groups[:-1]:
                bk = self.bank[2 + g]
                gi = g
                s8 = ss8[gi]
                nr = nrm[2 * gi + (t % 2)]
                nkey = ("nrm", 2 * gi + (t % 2))
                S.op("act", lambda e: e.activation(sq[:], bk[:], AF.Square),
                     reads=[("bank", 2 + g)], writes=["sqB"])
                S.op("dve", lambda e: e.tensor_reduce(s8[:], sq[:].rearrange("p (h d) -> p h d", d=64),
                                                      AX.X, ALU.add),
                     reads=["sqB"], writes=[("ss8", gi)])
                S.op("act", lambda e: e.activation(s8[:], s8[:], AF.Sqrt, bias=EPS, scale=1.0 / 64),
                     reads=[("ss8", gi)], writes=[("ss8", gi)])
                S.op("dve", lambda e: e.reciprocal(s8[:], s8[:]), reads=[("ss8", gi)], writes=[("ss8", gi)])
                S.op("dve", lambda e: e.tensor_tensor(
                    nr[:].rearrange("p (h d) -> p h d", d=64),
                    bk[:].rearrange("p (h d) -> p h d", d=64),
                    s8[:].unsqueeze(2).to_broadcast([128, 8, 64]), ALU.mult),
                    reads=[("bank", 2 + g), ("ss8", gi)], writes=[nkey])
                gt = qg if gi == 0 else kg
                S.op("pool", lambda e: e.tensor_tensor(
                    nr[:].rearrange("p (h d) -> p h d", d=64),
                    nr[:].rearrange("p (h d) -> p h d", d=64),
                    gt[:].unsqueeze(1).to_broadcast([128, 8, 64]), ALU.mult),
                    reads=[nkey, "qg" if gi == 0 else "kg"], writes=[nkey])
                if gi == 1 and own:
                    S.dma("sp", self.io["k_o"][orow:orow + 128, :], nr[:], reads=[nkey])
                yield
                nb = nbf[gi]
                S.op("act", lambda e: e.copy(nb[:], nr[:]), reads=[nkey], writes=[("nbf", gi)])
                tb = self.bank[5 + gi]
                for c in range(4):
                    S.op("pe", lambda e: e.matmul(tb[:, c * 128:(c + 1) * 128], nb[:, c * 128:(c + 1) * 128],
                                                  self.ident[:], start=True, stop=True),
                         reads=[("nbf", gi), "ident"], writes=[("bank", 5 + gi)])
                if gi == 0:
                    dst = self.QT[:, :, orow:orow + 128]
                    dkey = ("QT", t)
                else:
                    dst = self.KT[:, :, t * 128:(t + 1) * 128]
                    dkey = ("KT", t)
                S.op("dve", lambda e: e.tensor_copy(dst, tb[:].rearrange("p (c n) -> p c n", n=128)),
                     reads=[("bank", 5 + gi)], writes=[dkey])
                yield

        stage1(0)
        for t in range(33):
            g = tileB(t)
            next(g)
            if t + 1 < 33:
                stage1(t + 1)
            for _ in g:
                pass

    def attention(self, st2):
        nc, S, io = self.nc, self.S, self.io
        T2 = lambda name, shape, dt: st2.enter_context(nc.sbuf_tensor("sb_" + name, list(shape), dt))
        B = self.bank
        identb = self.ident
        MK = T2("MK", [128, 20, 512], BF16)
        for r in range(0, 20, 5):
            S.dma("pool", MK[:, r:r + 5, :], io["mk"][:, r:r + 5, :], writes=[("MK", r // 5)])
        MKK = [("MK", i) for i in range(4)]
        VO = [T2("VO%d" % i, [128, 32, 128], BF16) for i in range(2)]
        pv = T2("pvalid", [128, 64], F32)
        S.dma("sp", pv[:], io["pvalid"], writes=["pv"])
        for par in range(2):
            doff = 64 if par == 0 else 0
            S.op("pool", lambda e: e.tensor_copy(VO[par][:, 0:16, doff:doff + 64],
                                                 pv[:].unsqueeze(1).to_broadcast([128, 16, 64])),
                 reads=["pv"], writes=[("VO", par)])
            S.op("pool", lambda e: e.memset(VO[par][:, 16:32, doff:doff + 64], 1.0), writes=[("VO", par)])
        NSB = 6
        GRP = 3
        SBK = [0, 1, 2, 3, 6, 7]
        Eb = [T2("Eb%d" % i, [128, 512], BF16) for i in range(NSB)]
        Pb = [T2("Pb%d" % i, [128, 512], BF16) for i in range(NSB)]
        rD = [T2("rD%d" % i, [128, 512], F32) for i in range(2)]
        VK = [("Vb", t) for t in range(32)]
        its = []
        for h in range(8):
            for sb in range(4):
                for rel in range(20):
                    its.append((h, sb, rel))

        def geom(i):
            h, sb, rel = its[i]
            c, par = h // 2, h % 2
            po = par * 64
            kt = 4 * sb + rel
            j0, j1 = max(0, rel - 16), min(3, rel)
            cs = slice(128 * j0, 128 * (j1 + 1))
            qs = slice(4 * sb * 128 + 128 * j0, 4 * sb * 128 + 128 * (j1 + 1))
            return h, sb, rel, c, par, po, kt, j0, j1, cs, qs

        def rwS(i):
            h, sb, rel, c, par, po, kt, j0, j1, cs, qs = geom(i)
            return ([("KT", kt)] + [("QT", 16 + 4 * sb + j) for j in range(j0, j1 + 1)], [("bank", SBK[i % NSB])])

        def rwO(i):
            h, sb, rel, c, par, po, kt, j0, j1, cs, qs = geom(i)
            return ([("VO", par), ("Pb", i % NSB)], [("bank", 4 + (h * 4 + sb) % 2)])

        def emitS(i):
            h, sb, rel, c, par, po, kt, j0, j1, cs, qs = geom(i)
            sbk = i % NSB
            rd, wr = rwS(i)
            S.op("pe", lambda e: e.matmul(B[SBK[sbk]][:, cs], self.KT[po:po + 64, c, kt * 128:(kt + 1) * 128],
                                          self.QT[po:po + 64, c, qs], start=True, stop=True),
                 reads=rd, writes=wr)

        def emitE(i):
            h, sb, rel, c, par, po, kt, j0, j1, cs, qs = geom(i)
            sbk = i % NSB
            S.op("act", lambda e: e.activation(Eb[sbk][:, cs], B[SBK[sbk]][:, cs], AF.Exp, scale=0.125),
                 reads=[("bank", SBK[sbk])], writes=[("Eb", sbk)])
            S.op("dve",
                 lambda e: e.tensor_tensor(Pb[sbk][:, cs], Eb[sbk][:, cs], MK[:, rel, cs], ALU.mult),
                 reads=[("Eb", sbk)] + MKK, writes=[("Pb", sbk)])

        def emitO(i):
            h, sb, rel, c, par, po, kt, j0, j1, cs, qs = geom(i)
            sbk = i % NSB
            ob = 4 + (h * 4 + sb) % 2
            if sb == 0 and rel == 0:
                voff = 0 if par == 0 else 64
                S.op("pool", lambda e: e.tensor_copy(VO[par][:, :, voff:voff + 64], self.Vb[:, 0:32, h * 64:(h + 1) * 64]),
                     reads=VK, writes=[("VO", par)])
            S.op("pe", lambda e: e.matmul(B[ob][:, cs], VO[par][:, kt, :], Pb[sbk][:, cs],
                                          start=(rel == 0), stop=(rel == 19), skip_group_check=True),
                 reads=[("VO", par), ("Pb", sbk)], writes=[("bank", ob)])
            if rel == 19:
                dlo = 64 - po
                rd = rD[ob - 4]
                S.op("act", lambda e: e.activation(rd[po:po + 64, :], B[ob][dlo:dlo + 64, :], AF.Ln),
                     reads=[("bank", ob)], writes=[("rD", ob)])
                S.op("act", lambda e: e.activation(rd[po:po + 64, :], rd[po:po + 64, :], AF.Exp, scale=-1.0),
                     reads=[("rD", ob)], writes=[("rD", ob)])
                S.op("dve", lambda e: e.tensor_tensor(self.mixT[po:po + 64, 4 + c, 4 * sb * 128:4 * sb * 128 + 512],
                                                      B[ob][po:po + 64, :], rd[po:po + 64, :], ALU.mult),
                     reads=[("bank", ob), ("rD", ob)], writes=[("mixT", "b", h, sb)])

        n = len(its)
        groups = [list(range(i, min(i + GRP, n))) for i in range(0, n, GRP)]

        def emitSg(g):
            S.prewait("pe", [rwS(i) for i in groups[g]])
            for i in groups[g]:
                emitS(i)

        def emitOg(g):
            for i in groups[g]:
                h, sb, rel = its[i]
                if sb == 0 and rel == 0:
                    break
            else:
                S.prewait("pe", [rwO(i) for i in groups[g]])
            for i in groups[g]:
                emitO(i)

        ng = len(groups)
        emitSg(0)
        for g in range(ng):
            if g + 1 < ng:
                emitSg(g + 1)
            for i in groups[g]:
                emitE(i)
            if g >= 1:
                emitOg(g - 1)
        emitOg(ng - 1)
        S.op("pool", lambda e: e.tensor_copy(self.KTs[:], self.KT[:, :, NPRE + NOWN:NLOC]), reads=[("KT", 32)], writes=["KTs"])
        S.op("pool", lambda e: e.tensor_copy(self.QTs[:], self.QT[:, :, NOWN:NOUT]), reads=[("QT", 32)], writes=["QTs"])
        S.op("pool", lambda e: e.tensor_copy(self.Vbs[:], self.Vb[:, 32, :]), reads=[("Vb", 32)], writes=["Vbs"])

    def attention_smp(self, st2):
        nc, S, io = self.nc, self.S, self.io
        T2 = lambda name, shape, dt: st2.enter_context(nc.sbuf_tensor("sb_" + name, list(shape), dt))
        B = self.bank
        identb = self.ident
        Kc = [T2("Kc%d" % i, [128, 16, 512], BF16) for i in range(2)]
        Vc = [T2("Vc%d" % i, [128, 16, 512], BF16) for i in range(2)]
        KcT = [T2("KcT%d" % i, [128, 4, 2048], BF16) for i in range(2)]
        msmp = T2("msmp", [128, 16, 64], BF16)
        mnew = T2("mnew", [128, 16, 64], BF16)
        S.dma("pool", msmp[:], io["msmp"], writes=["msmp"])
        S.dma("pool", mnew[:], io["mnew"], writes=["mnew"])
        onesb = T2("onesb2", [128, 128], BF16)
        S.op("dve", lambda e: e.memset(onesb[:], 1.0), writes=["onesb2"])
        Qblk = T2("Qblk", [128, 4, 16], BF16)
        S.op("dve", lambda e: e.memset(Qblk[:], 0.0), writes=["Qblk"])
        Es = T2("Es", [128, 17, 64], BF16)
        Ps = T2("Ps", [128, 17, 64], BF16)
        rDs = T2("rDs", [128, 64], F32)

        def load(u):
            b = u % 2
            S.dma("pool", Kc[b][:], io["ck"][u].rearrange("(t p) f -> p t f", p=128), writes=[("Kc", b)])
            S.dma("pool", Vc[b][:], io["cv"][u].rearrange("(t p) f -> p t f", p=128), writes=[("Vc", b)])

        load(0)
        for u in range(16):
            b = u % 2
            if u + 1 < 16:
                load(u + 1)
            for kt in range(16):
                tb = 5 + kt % 2
                for c in range(4):
                    S.op("pe", lambda e: e.matmul(B[tb][:, c * 128:(c + 1) * 128], Kc[b][:, kt, c * 128:(c + 1) * 128],
                                                  identb[:], start=True, stop=True),
                         reads=[("Kc", b), "ident"], writes=[("bank", tb)])
                dst = KcT[b][:, :, kt * 128:(kt + 1) * 128]
                src = B[tb][:].rearrange("p (c n) -> p c n", c=4)
                if kt % 2 == 0:
                    S.op("act", lambda e: e.copy(dst, src), reads=[("bank", tb)], writes=[("KcT", b, kt)])
                else:
                    S.op("dve", lambda e: e.tensor_copy(dst, src), reads=[("bank", tb)], writes=[("KcT", b, kt)])
            S.op("pool", lambda e: e.tensor_copy(Qblk[0:64, :, 0:8], self.QTs[0:64, :, 8 * u:8 * u + 8]),
                 reads=["QTs"], writes=["Qblk"])
            S.op("pool", lambda e: e.tensor_copy(Qblk[64:128, :, 8:16], self.QTs[64:128, :, 8 * u:8 * u + 8]),
                 reads=["QTs"], writes=["Qblk"])
            for kt in range(16):
                sbk = kt // 8
                for c in range(4):
                    col = (kt % 8) * 64 + c * 16
                    S.op("pe", lambda e: e.matmul(B[sbk][:, col:col + 16], KcT[b][:, c, kt * 128:(kt + 1) * 128],
                                                  Qblk[:, c, :], start=True, stop=True),
                         reads=[("KcT", b, kt), "Qblk"], writes=[("bank", sbk)])
            for c in range(4):
                S.op("pe", lambda e: e.matmul(B[2][:, c * 16:(c + 1) * 16], self.KTs[:, c, :], Qblk[:, c, :],
                                              start=True, stop=True), reads=["KTs", "Qblk"], writes=[("bank", 2)])
            for sbk in range(2):
                S.op("act", lambda e: e.activation(Es[:, 8 * sbk:8 * sbk + 8, :].rearrange("p t n -> p (t n)"),
                                                   B[sbk][:], AF.Exp, scale=0.125),
                     reads=[("bank", sbk)], writes=[("Es", sbk)])
            S.op("act", lambda e: e.activation(Es[:, 16, :], B[2][:, 0:64], AF.Exp, scale=0.125),
                 reads=[("bank", 2)], writes=[("Es", 2)])
            S.op("dve", lambda e: e.tensor_tensor(Ps[:, 0:16, :], Es[:, 0:16, :], msmp[:], ALU.mult),
                 reads=[("Es", 0), ("Es", 1), "msmp"], writes=["Ps"])
            S.op("dve", lambda e: e.tensor_tensor(Ps[:, 16, :], Es[:, 16, :], mnew[:, u, :], ALU.mult),
                 reads=[("Es", 2), "mnew"], writes=["Ps"])
            first = True
            for kt in range(17):
                for c in range(4):
                    lhsT = Vc[b][:, kt, c * 128:(c + 1) * 128] if kt < 16 else self.Vbs[:, c * 128:(c + 1) * 128]
                    S.op("pe", lambda e: e.matmul(B[3][:, c * 16:(c + 1) * 16], lhsT, Ps[:, kt, c * 16:(c + 1) * 16],
                                                  start=first, stop=False, skip_group_check=True),
                         reads=[("Vc", b), "Vbs", "Ps"], writes=[("bank", 3)])
                    first = False
                S.op("pe", lambda e: e.matmul(B[3][:, 64:128], onesb[:], Ps[:, kt, :], start=False, stop=(kt == 16),
                                              skip_group_check=True),
                     reads=["onesb2", "Ps"], writes=[("bank", 3)])
            S.op("dve", lambda e: e.reciprocal(rDs[:], B[3][:, 64:128]), reads=[("bank", 3)], writes=["rDs"])
            for hh in range(2):
                rows = slice(hh * 64, hh * 64 + 64)
                S.op("dve", lambda e: e.tensor_tensor(
                    self.mixT[rows, 4:8, NOWN + 8 * u:NOWN + 8 * u + 8],
                    B[3][rows, 0:64].rearrange("p (c x) -> p c x", c=4)[:, :, hh * 8:hh * 8 + 8],
                    rDs[rows, :].rearrange("p (c x) -> p c x", c=4)[:, :, hh * 8:hh * 8 + 8], ALU.mult),
                    reads=[("bank", 3), "rDs"], writes=[("mixT", "s", u, hh)])

    def mlp(self, st2):
        nc, S, io = self.nc, self.S, self.io
        T2 = lambda name, shape, dt: st2.enter_context(nc.sbuf_tensor("sb_" + name, list(shape), dt))
        B = self.bank
        identb = self.ident
        wup = T2("wup", [128, 8, 4096], BF16)
        wdn = T2("wdn", [128, 32, 1024], BF16)
        st3 = contextlib.ExitStack()
        wo = st3.enter_context(nc.sbuf_tensor("sb_wo", [128, 8, 1024], BF16))
        wosrc = io["w_o"].rearrange("(kc p) n -> p kc n", p=128)
        for kc in range(8):
            S.dma("pool", wo[:, kc, :], wosrc[:, kc, :], writes=[("wo", kc)])
        g2 = self.g1
        S.dma("sp", g2[:], io["g2bc"], writes=["g1"])
        usrc = io["w_up"].rearrange("(kc p) n -> p kc n", p=128)
        dsrc = io["w_down"].rearrange("(f p) n -> p f n", p=128)
        for kc in range(8):
            S.dma("pool", wup[:, kc, :], usrc[:, kc, :], writes=[("wup", kc)])
        for f in range(0, 32, 4):
            S.dma("pool", wdn[:, f:f + 4, :], dsrc[:, f:f + 4, :], writes=[("wdn", f // 4)])
        h1s = self.xt
        tiles = list(range(16, 33))

        def mixkeys(t):
            if t < 32:
                i = t - 16
                return [("mixT", t)] + [("mixT", "b", h, i // 4) for h in range(8)]
            return [("mixT", 32)] + [("mixT", "s", u, hh) for u in range(16) for hh in range(2)]

        self.load_x(tiles[0], 0)

        def stage3a(n, t):
            b = n % 2
            if n + 1 < len(tiles):
                self.load_x(tiles[n + 1], (n + 1) % 2)
            orow = (t - 16) * 128
            cols = slice(orow, orow + 128)
            mk = mixkeys(t)
            for half in range(2):
                for kc in range(8):
                    S.op("pe", lambda e: e.matmul(B[2 + half][:], self.mixT[:, kc, cols], wo[:, kc, half * 512:(half + 1) * 512],
                                                  start=(kc == 0), stop=(kc == 7)),
                         reads=mk + [("wo", kc)], writes=[("bank", 2 + half)])
            h1 = h1s[b]
            for half in range(2):
                hsl = slice(half * 512, (half + 1) * 512)
                S.op("dve", lambda e: e.tensor_tensor(h1[:, hsl], self.xt[b][:, hsl], B[2 + half][:], ALU.add),
                     reads=[("bank", 2 + half), ("xt", b)], writes=[("xt", b)])
            S.dma("sp", io["y_o"][orow:orow + 128, :], h1[:], reads=[("xt", b)], writes=[("h1d", t)])
            ss, rs, xn = self.ss[b], self.rs[b], self.xn[b]
            S.op("act", lambda e: e.activation(self.junk[:], h1[:], AF.Square, accum_out=ss[:]),
                 reads=[("xt", b)], writes=["junk", ("ss", b)])
            S.op("act", lambda e: e.activation(rs[:], ss[:], AF.Sqrt, bias=EPS, scale=1.0 / D),
                 reads=[("ss", b)], writes=[("rs", b)])
            S.op("dve", lambda e: e.reciprocal(rs[:], rs[:]), reads=[("rs", b)], writes=[("rs", b)])
            S.op("dve", lambda e: e.scalar_tensor_tensor(xn[:], h1[:], rs[:, 0:1], g2[:], ALU.mult, ALU.mult),
                 reads=[("xt", b), ("rs", b), "g1"], writes=[("xn", b)])
            yield
            for kc in range(8):
                bk = B[kc // 4]
                S.op("pe", lambda e: e.matmul(bk[:, (kc % 4) * 128:(kc % 4 + 1) * 128],
                                              xn[:, kc * 128:(kc + 1) * 128], identb[:], start=True, stop=True),
                     reads=[("xn", b), "ident"], writes=[("bank", kc // 4)])
            S.op("act", lambda e: e.copy(self.mixT[:, 0:4, cols], B[0][:].rearrange("p (a n) -> p a n", a=4)),
                 reads=[("bank", 0)], writes=mk)
            S.op("dve", lambda e: e.tensor_copy(self.mixT[:, 4:8, cols], B[1][:].rearrange("p (a n) -> p a n", a=4)),
                 reads=[("bank", 1)], writes=mk)

        gens3a = [stage3a(n, t) for n, t in enumerate(tiles)]
        next(gens3a[0])
        for n in range(len(tiles)):
            if n + 1 < len(tiles):
                next(gens3a[n + 1])
            for _ in gens3a[n]:
                pass
        S.barrier()
        st3.close()
        hidT = T2("hidT", [128, 32, 128], BF16)
        rl = [T2("rl%d" % i, [128, 512], BF16) for i in range(2)]
        for n, t in enumerate(tiles):
            b = n % 2
            orow = (t - 16) * 128
            cols = slice(orow, orow + 128)
            mk = mixkeys(t)
            h1 = h1s[b]
            S.dma("sp", h1[:], io["y_o"][orow:orow + 128, :], reads=[("h1d", t)], writes=[("xt", b)])
            for fg in range(8):
                bk = 2 + fg % 2
                for q in range(4):
                    f = 4 * fg + q
                    for kc in range(8):
                        S.op("pe", lambda e: e.matmul(B[bk][:, q * 128:(q + 1) * 128], wup[:, kc, f * 128:(f + 1) * 128],
                                                      self.mixT[:, kc, cols], start=(kc == 0), stop=(kc == 7)),
                             reads=mk + [("wup", kc)], writes=[("bank", bk)])
                r = rl[fg % 2]
                S.op("act", lambda e: e.activation(r[:], B[bk][:], AF.Relu), reads=[("bank", bk)], writes=[("rl", fg % 2)])
                S.op("pool" if fg % 2 else "dve",
                     lambda e: e.tensor_tensor(hidT[:, 4 * fg:4 * fg + 4, :].rearrange("p a n -> p (a n)"), r[:], r[:], ALU.mult),
                     reads=[("rl", fg % 2)], writes=[("hidT", fg)])
            for half in range(2):
                for f in range(32):
                    S.op("pe", lambda e: e.matmul(B[4 + half][:], hidT[:, f, :], wdn[:, f, half * 512:(half + 1) * 512],
                                                  start=(f == 0), stop=(f == 31)),
                         reads=[("hidT", f // 4), ("wdn", f // 4)], writes=[("bank", 4 + half)])
            for half in range(2):
                hsl = slice(half * 512, (half + 1) * 512)
                S.op("dve", lambda e: e.tensor_tensor(h1[:, hsl], h1[:, hsl], B[4 + half][:], ALU.add),
                     reads=[("bank", 4 + half), ("xt", b)], writes=[("xt", b)])
            S.dma("sp", io["y_o"][orow:orow + 128, :], h1[:], reads=[("xt", b), ("h1d", t)], writes=[("yd", t)])

    def pass_A(self, st2):
        nc, S, io = self.nc, self.S, self.io
        T2 = lambda name, shape, dt: st2.enter_context(nc.sbuf_tensor("sb_" + name, list(shape), dt))
        B = self.bank
        def bq(i, q=None):
            return [("b", i, k) for k in range(4)] if q is None else [("b", i, q)]
        wA = T2("wA", [128, 8, 2056], BF16)
        wsrc = io["w_inA"].rearrange("(kc p) n -> p kc n", p=128)
        for kc in range(8):
            S.dma("pool", wA[:, kc, :], wsrc[:, kc, :], writes=[("wA", kc)])
        cm = T2("cm", [128, 11, 128], F32)
        S.dma("sp", cm[:], io["cmask"], writes=["cm"])
        cw = T2("cw", [128, 12, 4], F32)
        S.dma("sp", cw[:], io["cwT"], writes=["cw"])
        nea = T2("nea", [128, 4], F32)
        dtb = T2("dtb", [128, 4], F32)
        dng = T2("dng", [128, 128], F32)
        S.dma("sp", nea[:], io["alogbc"], writes=["nea"])
        S.dma("sp", dtb[:], io["dtbbc"], writes=["dtb"])
        S.dma("sp", dng[:], io["dngbc"], writes=["dng"])
        S.op("act", lambda e: e.activation(nea[:], nea[:], AF.Exp), reads=["nea"], writes=["nea"])
        S.op("dve", lambda e: e.tensor_scalar(nea[:], nea[:], -1.0, None, ALU.mult), reads=["nea"], writes=["nea"])
        onesf = T2("onesf", [128, 128], F32)
        onesb = T2("onesb", [128, 128], BF16)
        S.op("dve", lambda e: e.memset(onesf[:], 1.0), writes=["onesf"])
        S.op("dve", lambda e: e.memset(onesb[:], 1.0), writes=["onesb"])
        identf = cm[:, 9, :]
        identb = self.ident
        uT = [T2("uT%d" % i, [128, 12, 131], F32) for i in range(2)]
        acc = T2("acc", [128, 12, 128], F32)
        ctmp = T2("ctmp", [128, 128], F32)
        stP = contextlib.ExitStack()
        TP = lambda name, shape, dt: stP.enter_context(nc.sbuf_tensor("sb_" + name, list(shape), dt))
        PA = lambda i: T2 if i == 0 else TP
        sqc = T2("sqc", [128, 8, 128], BF16)
        rst = T2("rst", [128, 8, 128], F32)
        (BAS0, BAS1, EB, BETA, LNEB, AD, G_, GL_, EG, EGLG, EGL, NBG, NEGG, GLN, SSO, RSO) = range(16)
        diagG = T2("diagG", [128, 4, 128], F32)
        diagL = T2("diagL", [128, 4, 128], F32)
        tU = T2("tU", [128, 4, 128], F32)
        tS = T2("tS", [128, 4, 128], F32)
        tL = T2("tL", [128, 4, 128], F32)
        E1, E2, E3 = tU, tS, tL
        Mm = [T2("Mm%d" % i, [128, 4, 128], BF16) for i in range(2)]
        MTr = [T2("MT%d" % i, [128, 4, 128], BF16) for i in range(2)]
        v1f = T2("v1f", [128, 4, 128], F32)
        R32 = T2("R32", [128, 4, 128], F32)
        t1 = T2("t1", [128, 4, 128], F32)
        v1b = T2("v1b", [128, 4, 128], BF16)
        r1b = T2("r1b", [128, 4, 128], BF16)
        S32 = T2("S32", [128, 4, 128], F32)
        Sbf = T2("Sbf", [128, 4, 128], BF16)
        Rr = T2("Rr", [128, 4, 128], BF16)
        vnew = T2("vnew", [128, 4, 128], BF16)
        tmpo = T2("tmpo", [128, 4, 128], F32)
        o32 = T2("o32", [128, 4, 128], F32)
        sqo = t1[:].rearrange("p h n -> p (h n)")
        T1K = [("t1", h) for h in range(4)]
        obf = T2("obf", [128, 512], BF16)
        cps = acc[:].rearrange("p c n -> p (c n)")[0:48, :]
        ACCK = [("acc", ch) for ch in range(12)]
        stok = uT[0][:].rearrange("p c n -> p (c n)")[0:48, 0:1536]
        cTs, kds, vbs, sms, PTs, LT0s, qkTs, zss = [], [], [], [], [], [], [], []

        def alloc_par(i):
            A_ = PA(i)
            cTs.append(A_("cT%d" % i, [128, 12, 128], BF16))
            kds.append(A_("kd%d" % i, [128, 4, 128], BF16))
            vbs.append(A_("vb%d" % i, [128, 4, 128], BF16))
            sms.append(A_("sm%d" % i, [128, 16, 4], F32))
            PTs.append([A_("PT%d_%d" % (i, j), [128, 4, 128], BF16) for j in range(2)])
            LT0s.append(A_("LT0_%d" % i, [128, 4, 128], BF16))
            qkTs.append(A_("qkT%d" % i, [128, 4, 128], BF16))
            zss.append(A_("zs%d" % i, [128, 512], F32))

        alloc_par(0)
        uS = cs3 = Ssm32b = Ssmbb = kSTs = None
        gsel = T2("gsel", [128, 4, 16], F32)
        eglb = T2("eglb", [128, 4, 16], F32)
        vblk = acc[:].rearrange("p c n -> p (c n)").bitcast(BF16)[:, 0:2048].rearrange("p (u v) -> p u v", u=16)
        S.op("dve", lambda e: e.memset(S32[:], 0.0), writes=[("S32", h) for h in range(4)])
        S.op("dve", lambda e: e.memset(Sbf[:], 0.0), writes=[("Sbf", h) for h in range(4)])
        S.op("dve", lambda e: e.memset(uT[1][:, :, 128:131], 0.0), writes=[("uT", 1)])
        def load_smp_state(h):
            S.dma("sp", Ssm32b[h % 2][:], io["sdelta"][:, h].rearrange("u k v -> k u v"), writes=[("Ssm32", h % 2)])
            S.dma("pool", Ssmbb[h % 2][:], io["sdelta"][:, h].rearrange("u k v -> k u v"), writes=[("Ssmb", h % 2)])

        def tileA(t):
            p = t % 2
            N = lambda k: (k, "par", p)
            cT, kd, vb, sm, PT, qkT, zs = cTs[p], kds[p], vbs[p], sms[p], PTs[p], qkTs[p], zss[p]
            MT = [MTr[0], MTr[1], LT0s[p]]
            MK_ = lambda i: N(("MT", 2)) if i == 2 else ("MT", i)

            def sc(i, h=None):
                return sm[:, i, :] if h is None else sm[:, i, h:h + 1]

            if t + 2 < 33:
                self.load_x(t + 2, t % 2)
            if t + 1 < 33:
                self.norm_T((t + 1) % 2, self.g1, "g1", part=1, lnexp=True)
            yield
            mode = "pre" if t < 16 else ("own" if t < 32 else "smp")
            own = mode != "pre"
            smp = mode == "smp"
            b = t % 2
            xnT = self.xnT[b]
            chs = list(range(12)) if own else list(range(4, 12))
            ub = uT[b]
            orow = (t - 16) * 128
            msk = (1, 2, 6, 7, 8) if smp else (0, None, 3, 4, 5)
            tri = cm[:, msk[0], :]
            glones = cm[:, 2, :] if smp else onesf[:]
            nm_ui, nm_us, nm_ls = cm[:, msk[2], :], cm[:, msk[3], :], cm[:, msk[4], :]
            nlev = 3 if smp else 7
            chs_proj = list(range(12)) if t == 15 else chs
            for ch in chs_proj:
                bi = (2 + ch // 4) if smp else (2 + (ch // 4) % 2)
                col = (ch % 4) * 128
                for kc in range(8):
                    S.op("pe", lambda e: e.matmul(B[bi][:, col:col + 128], wA[:, kc, ch * 128:(ch + 1) * 128],
                                                  xnT[:, kc, :], start=(kc == 0), stop=(kc == 7)),
                         reads=[("xnT", b), ("wA", kc)], writes=bq(bi, ch % 4))
                if not smp and ch % 4 == 3:
                    gi = ch // 4
                    S.op("act", lambda e: e.copy(ub[:, 4 * gi:4 * gi + 4, 3:131],
                                                 B[bi][:].rearrange("p (c n) -> p c n", c=4)),
                         reads=bq(bi), writes=[("uT", b)])
                    S.op("dve", lambda e: e.tensor_copy(ubf[:, 4 * gi:4 * gi + 4, 3:131],
                                                        B[bi][:].rearrange("p (c n) -> p c n", c=4)),
                         reads=bq(bi), writes=[("ubf", gi)])
            if t + 1 < 33:
                self.norm_T((t + 1) % 2, self.g1, "g1", part=2)
            if smp:
                S.dma("sp", stok, io["sconv"].rearrange("u r c -> (u r) c"), writes=[("uT", 0)])
                load_smp_state(0)
                load_smp_state(1)
                for ch in range(12):
                    bi = 5 + ch // 6
                    col = (ch % 6) * 48
                    S.op("pe", lambda e: e.matmul(B[bi][:, col:col + 48], stok[:, ch * 128:(ch + 1) * 128],
                                                  identf[0:48, 0:48], start=True, stop=True),
                         reads=[("uT", 0), "cm"], writes=bq(bi))
                for gi in range(2):
                    S.op("dve", lambda e: e.tensor_copy(
                        uS[:, 6 * gi:6 * gi + 6, :, 0:3],
                        B[5 + gi][:, 0:288].rearrange("p (c u r) -> p c u r", c=6, u=16)),
                        reads=bq(5 + gi), writes=["uS"])
                for gi in range(3):
                    S.op("act", lambda e: e.copy(
                        uS[:, 4 * gi:4 * gi + 4, :, 3:11],
                        B[2 + gi][:].rearrange("p (c u r) -> p c u r", c=4, u=16)),
                        reads=bq(2 + gi), writes=["uS"])
            else:
                up = uT[1 - b]
                S.op("dve", lambda e: e.tensor_copy(ub[:, :, 0:3], up[:, :, 128:131]),
                     reads=[("uT", 1 - b)], writes=[("uT", b)])
                S.op("dve", lambda e: e.tensor_copy(ubf[:, :, 0:3], up[:, :, 128:131]),
                     reads=[("uT", 1 - b)], writes=[("ubf", "c")])
            for kc in range(8):
                S.op("pe", lambda e: e.matmul(B[0][:, 0:8], xnT[:, kc, :], wA[:, kc, 2048:2056],
                                              start=(kc == 0), stop=(kc == 7)),
                     reads=[("xnT", b), ("wA", kc)], writes=bq(0, 0))
            if own:
                for kc in range(8):
                    S.op("pe", lambda e: e.matmul(B[1][:], xnT[:, kc, :], wA[:, kc, 1536:2048],
                                                  start=(kc == 0), stop=(kc == 7)),
                         reads=[("xnT", b), ("wA", kc)], writes=bq(1))
            yield
            if not smp:
                for gi in sorted(set(ch // 4 for ch in chs)):
                    bi = 2 + gi % 2
                    for ch in range(4 * gi, 4 * gi + 4):
                        for i in range(4):
                            S.op("pe", lambda e: e.matmul(B[bi][:, (ch % 4) * 128:(ch % 4 + 1) * 128], Dg[:, 4 * ch + i, :],
                                                          ubf[:, ch, i:i + 128], start=(i == 0), stop=(i == 3)),
                                 reads=["Dg", ("ubf", gi), ("ubf", "c")], writes=bq(bi, ch % 4))
                    S.op("act", lambda e: e.activation(cT[:, 4 * gi:4 * gi + 4, :].rearrange("p c n -> p (c n)"), B[bi][:], AF.Silu),
                         reads=bq(bi), writes=[N("cT")])
            for n, ch in enumerate(chs if smp else []):
                eng = "dve"
                for i in range(4):
                    if smp:
                        src = uS[:, ch, :, i:i + 8]
                        dst = acc[:, ch, :].rearrange("p (u r) -> p u r", u=16)
                        skey = "uS"
                    else:
                        src = ub[:, ch, i:i + 128]
                        dst = acc[:, ch, :]
                        skey = ("uT", b)
                    if i == 0:
                        S.op(eng, lambda e: e.tensor_scalar(dst, src, cw[:, ch, 0:1], None, ALU.mult),
                             reads=[skey, "cw"], writes=[("acc", ch)])
                    elif eng == "dve":
                        S.op(eng, lambda e: e.scalar_tensor_tensor(dst, src, cw[:, ch, i:i + 1], dst,
                                                                   ALU.mult, ALU.add),
                             reads=[skey, "cw", ("acc", ch)], writes=[("acc", ch)])
                    else:
                        tdst = ctmp[:].rearrange("p (u r) -> p u r", u=16) if smp else ctmp[:]
                        S.op(eng, lambda e: e.tensor_scalar(tdst, src, cw[:, ch, i:i + 1], None, ALU.mult),
                             reads=[skey, "cw"], writes=["ctmp"])
                        S.op(eng, lambda e: e.tensor_tensor(dst, dst, tdst, ALU.add),
                             reads=["ctmp", ("acc", ch)], writes=[("acc", ch)])
            if smp:
                S.op("act", lambda e: e.activation(cT[:, chs[0]:12, :], acc[:, chs[0]:12, :], AF.Silu),
                     reads=[("acc", ch) for ch in chs], writes=[N("cT")])
            if own:
                S.op("act", lambda e: e.activation(zs[:], B[1][:], AF.Silu), reads=bq(1), writes=[N("zs")])
            yield
            lo = 0 if own else 4
            S.op("act", lambda e: e.activation(sqc[:, lo:8, :], cT[:, lo:8, :], AF.Square),
                 reads=[N("cT")], writes=["sqc"])
            for ch in range(lo, 8):
                bi = 2 + ch // 4
                S.op("pe", lambda e: e.matmul(B[bi][:, (ch % 4) * 128:(ch % 4 + 1) * 128], onesb[:], sqc[:, ch, :],
                                              start=True, stop=True),
                     reads=["sqc", "onesb"], writes=bq(bi, ch % 4))
            for gi in range(lo // 4, 2):
                rv = rst[:, 4 * gi:4 * gi + 4, :].rearrange("p c n -> p (c n)")
                S.op("act", lambda e: e.activation(rv, B[2 + gi][:], AF.Ln, bias=EPS, scale=1.0),
                     reads=bq(2 + gi), writes=[("rst", gi)])
                S.op("act", lambda e: e.activation(rv, rv, AF.Exp, scale=-0.5),
                     reads=[("rst", gi)], writes=[("rst", gi)])
            if own:
                S.op("dve", lambda e: e.scalar_tensor_tensor(cT[:, 0:4, :], rst[:, 0:4, :], 128.0 ** -0.5,
                                                             cT[:, 0:4, :], ALU.mult, ALU.mult),
                     reads=[("rst", 0), N("cT")], writes=[N("cT")])
            S.op("dve", lambda e: e.tensor_tensor(cT[:, 4:8, :], cT[:, 4:8, :], rst[:, 4:8, :], ALU.mult),
                 reads=[("rst", 1), N("cT")], writes=[N("cT")])
            yield
            for h in range(4):
                S.op("pe", lambda e: e.matmul(B[2][:, h * 128:(h + 1) * 128], cT[:, 4 + h, :], identb[:],
                                              start=True, stop=True), reads=[N("cT"), "ident"], writes=bq(2, h))
                S.op("pe", lambda e: e.matmul(B[3][:, h * 128:(h + 1) * 128], cT[:, 8 + h, :], identb[:],
                                              start=True, stop=True), reads=[N("cT"), "ident"], writes=bq(3, h))
            S.op("act", lambda e: e.copy(sm[:, BAS0:BAS1 + 1, :].rearrange("p a b -> p (a b)"), B[0][:, 0:8]),
                 reads=bq(0, 0), writes=[N("bas")])
            S.op("act", lambda e: e.activation(sc(EB), sc(BAS0), AF.Exp, scale=-1.0), reads=[N("bas")], writes=[N("eb")])
            S.op("dve", lambda e: e.tensor_scalar(sc(EB), sc(EB), 1.0, None, ALU.add), reads=[N("eb")], writes=[N("eb")])
            S.op("dve", lambda e: e.reciprocal(sc(BETA), sc(EB)), reads=[N("eb")], writes=[N("beta")])
            S.op("act", lambda e: e.activation(sc(LNEB), sc(EB), AF.Ln), reads=[N("eb")], writes=[N("lneb")])
            S.op("dve", lambda e: e.tensor_tensor(sc(AD), sc(BAS1), dtb[:], ALU.add), reads=[N("bas"), "dtb"], writes=[N("ad")])
            S.op("act", lambda e: e.activation(sc(AD), sc(AD), AF.Exp), reads=[N("ad")], writes=[N("ad")])
            S.op("act", lambda e: e.activation(sc(AD), sc(AD), AF.Ln, bias=1.0), reads=[N("ad")], writes=[N("ad")])
            S.op("dve", lambda e: e.tensor_tensor(sc(AD), sc(AD), nea[:], ALU.mult), reads=[N("ad"), "nea"], writes=[N("ad")])
            S.op("pe", lambda e: e.matmul(B[0][:, 8:12], tri, sc(AD), start=True, stop=True),
                 reads=[N("ad"), "cm"], writes=bq(0, 0))
            S.op("pe", lambda e: e.matmul(B[0][:, 12:16], glones, sc(AD), start=True, stop=True),
                 reads=[N("ad"), "cm", "onesf"], writes=bq(0, 0))
            S.op("act", lambda e: e.copy(sm[:, G_:GL_ + 1, :].rearrange("p a b -> p (a b)"), B[0][:, 8:16]),
                 reads=bq(0, 0), writes=[N("G")])
            S.op("act", lambda e: e.activation(sc(EG), sc(G_), AF.Exp), reads=[N("G")], writes=[N("eG")])
            S.op("dve", lambda e: e.tensor_tensor(sc(EGLG), sc(GL_), sc(G_), ALU.subtract), reads=[N("G")], writes=[N("eglG")])
            S.op("act", lambda e: e.activation(sc(EGLG), sc(EGLG), AF.Exp), reads=[N("eglG")], writes=[N("eglG")])
            S.op("act", lambda e: e.activation(sc(EGL), sc(GL_), AF.Exp), reads=[N("G")], writes=[N("egl")])
            S.op("dve", lambda e: e.scalar_tensor_tensor(sc(NBG), sc(BETA), -1.0, sc(EG), ALU.mult, ALU.mult),
                 reads=[N("beta"), N("eG")], writes=[N("nbG")])
            S.op("dve", lambda e: e.tensor_scalar(sc(NEGG), sc(G_), -1.0, None, ALU.mult), reads=[N("G")], writes=[N("negG")])
            S.op("dve", lambda e: e.tensor_tensor(sc(GLN), sc(G_), sc(LNEB), ALU.subtract),
                 reads=[N("G"), N("lneb")], writes=[N("GLN")])
            for h in range(4):
                S.op("act", lambda e: e.activation(kd[:, h, :], B[2][:, h * 128:(h + 1) * 128], AF.Copy,
                                                   scale=sc(EGLG, h)), reads=bq(2, h) + [N("eglG")], writes=[N(("kd", h))])
                S.op("dve", lambda e: e.tensor_scalar(vb[:, h, :], B[3][:, h * 128:(h + 1) * 128], sc(BETA, h), None,
                                                      ALU.mult), reads=bq(3, h) + [N("beta")], writes=[N(("vb", h))])
            yield
            S.op("dve", lambda e: e.tensor_tensor(diagG[:], identf.unsqueeze(1).to_broadcast([128, 4, 128]),
                                                  sc(G_).unsqueeze(2).to_broadcast([128, 4, 128]), ALU.mult),
                 reads=[N("G"), "cm"], writes=["diagG"])
            S.op("pool", lambda e: e.tensor_tensor(diagL[:], identf.unsqueeze(1).to_broadcast([128, 4, 128]),
                                                   sc(GLN).unsqueeze(2).to_broadcast([128, 4, 128]), ALU.mult),
                 reads=[N("GLN"), "cm"], writes=["diagL"])
            for h in range(4):
                S.op("pe", lambda e: e.matmul(B[0][:, h * 128:(h + 1) * 128], onesf[:], diagG[:, h, :],
                                              start=True, stop=True), reads=["diagG", "onesf"], writes=bq(0, h))
                S.op("pe", lambda e: e.matmul(B[1][:, h * 128:(h + 1) * 128], onesf[:], diagL[:, h, :],
                                              start=True, stop=True), reads=["diagL", "onesf"], writes=bq(1, h))
            b2v = B[0][:].rearrange("p (h n) -> p h n", h=4)
            b3v = B[1][:].rearrange("p (h n) -> p h n", h=4)
            if own:
                S.op("dve", lambda e: e.tensor_tensor(tU[:], b2v, nm_ui.unsqueeze(1).to_broadcast([128, 4, 128]), ALU.add),
                     reads=bq(0) + ["cm"], writes=["tU"])
            S.op("dve", lambda e: e.tensor_tensor(tS[:], b3v, nm_us.unsqueeze(1).to_broadcast([128, 4, 128]), ALU.add),
                 reads=bq(1) + ["cm"], writes=["tS"])
            S.op("dve", lambda e: e.scalar_tensor_tensor(tL[:], b2v, -1.0, nm_ls.unsqueeze(1).to_broadcast([128, 4, 128]),
                                                         ALU.mult, ALU.add), reads=bq(0) + ["cm"], writes=["tL"])
            yield
            for h in range(4):
                if own:
                    S.op("act", lambda e: e.activation(E1[:, h, :], tU[:, h, :], AF.Exp, bias=sc(NEGG, h)),
                         reads=["tU", N("negG")], writes=[("E1", h), "tU"])
                S.op("act", lambda e: e.activation(E2[:, h, :], tS[:, h, :], AF.Exp, bias=sc(NEGG, h)),
                     reads=["tS", N("negG")], writes=[("E2", h), "tS"])
                S.op("act", lambda e: e.activation(E3[:, h, :], tL[:, h, :], AF.Exp, bias=sc(GLN, h)),
                     reads=["tL", N("GLN")], writes=[("E3", h), "tL"])
            for h in range(4):
                S.op("pe", lambda e: e.matmul(B[2][:, h * 128:(h + 1) * 128], cT[:, 4 + h, :], cT[:, 4 + h, :],
                                              start=True, stop=True), reads=[N("cT")], writes=bq(2, h))
                if own:
                    S.op("pe", lambda e: e.matmul(B[3][:, h * 128:(h + 1) * 128], cT[:, 4 + h, :], cT[:, h, :],
                                                  start=True, stop=True), reads=[N("cT")], writes=bq(3, h))
            b4v = B[2][:].rearrange("p (h n) -> p h n", h=4)
            b5v = B[3][:].rearrange("p (h n) -> p h n", h=4)
            E3k = [("E3", h) for h in range(4)]
            E2k = [("E2", h) for h in range(4)]
            E1k = [("E1", h) for h in range(4)]
            S.op("dve", lambda e: e.scalar_tensor_tensor(Mm[0][:], b4v, -1.0, E3[:], ALU.mult, ALU.mult),
                 reads=bq(2) + E3k, writes=[("Mm", 0)])
            S.op("dve", lambda e: e.scalar_tensor_tensor(MT[2][:], b4v, -1.0, E2[:], ALU.mult, ALU.mult),
                 reads=bq(2) + E2k, writes=[N(("MT", 2))])
            if own:
                S.op("dve", lambda e: e.tensor_tensor(qkT[:], b5v, E1[:], ALU.mult), reads=bq(3) + E1k, writes=[N("qkT")])
            S.op("pool", lambda e: e.tensor_tensor(PT[0][:], MT[2][:], identb[:].unsqueeze(1).to_broadcast([128, 4, 128]),
                                                   ALU.add), reads=[N(("MT", 2)), "ident"], writes=[N(("PT", 0))])
            yield
            mcur, tcur = 0, 2
            pcur = 0
            for k in range(1, nlev):
                mnxt = 1 - mcur
                tnxt = 0 if tcur != 0 else 1
                last = (k == nlev - 1)
                for h in range(4):
                    hs = slice(h * 128, (h + 1) * 128)
                    S.op("pe", lambda e: e.matmul(B[0][:, hs], MT[tcur][:, h, :], Mm[mcur][:, h, :], start=True, stop=True),
                         reads=[MK_(tcur), ("Mm", mcur)], writes=bq(0, h))
                    if not last:
                        S.op("pe", lambda e: e.matmul(B[1][:, hs], Mm[mcur][:, h, :], MT[tcur][:, h, :], start=True, stop=True),
                             reads=[MK_(tcur), ("Mm", mcur)], writes=bq(1, h))
                S.op("act", lambda e: e.copy(Mm[mnxt][:].rearrange("p h n -> p (h n)"), B[0][:]),
                     reads=bq(0), writes=[("Mm", mnxt)])
                if not last:
                    S.op("dve", lambda e: e.tensor_copy(MT[tnxt][:].rearrange("p h n -> p (h n)"), B[1][:]),
                         reads=bq(1), writes=[MK_(tnxt)])
                pn = 1 - pcur
                for h in range(4):
                    hs = slice(h * 128, (h + 1) * 128)
                    S.op("pe", lambda e: e.matmul(B[2][:, hs], identb[:], PT[pcur][:, h, :], start=True, stop=False),
                         reads=[N(("PT", pcur)), "ident"], writes=bq(2, h))
                    S.op("pe", lambda e: e.matmul(B[2][:, hs], Mm[mnxt][:, h, :], PT[pcur][:, h, :], start=False, stop=True),
                         reads=[N(("PT", pcur)), ("Mm", mnxt)], writes=bq(2, h))
                S.op("act" if k % 2 else "dve",
                     (lambda e: e.copy(PT[pn][:].rearrange("p h n -> p (h n)"), B[2][:])) if k % 2 else
                     (lambda e: e.tensor_copy(PT[pn][:].rearrange("p h n -> p (h n)"), B[2][:])),
                     reads=bq(2), writes=[N(("PT", pn))])
                yield
                mcur = mnxt
                tcur = tnxt
                pcur = pn
            Ainv = PT[pcur]
            akey = N(("PT", pcur))
            yield "SCAN"
            KS, QS, VS, OS = slice(0, 128), slice(128, 256), slice(256, 384), slice(384, 512)
            if smp:
                for h in range(4):
                    for u in range(16):
                        us = slice(h * 128 + 8 * u, h * 128 + 8 * u + 8)
                        S.op("pe", lambda e: e.matmul(B[2][:, us], Ssmbb[h % 2][:, u, :], cT[:, 4 + h, 8 * u:8 * u + 8],
                                                      start=True, stop=True), reads=[N("cT"), ("Ssmb", h % 2)], writes=bq(2, h))
                        S.op("pe", lambda e: e.matmul(B[3][:, us], Ssmbb[h % 2][:, u, :], cT[:, h, 8 * u:8 * u + 8],
                                                      start=True, stop=True), reads=[N("cT"), ("Ssmb", h % 2)], writes=bq(3, h))
                    if h + 2 < 4:
                        S.dma("pool", Ssmbb[h % 2][:], io["sdelta"][:, h + 2].rearrange("u k v -> k u v"),
                              writes=[("Ssmb", h % 2)])
                S.op("act", lambda e: e.copy(kSTs[:, 0:4, :].rearrange("p h n -> p (h n)"), B[2][:]),
                     reads=bq(2), writes=["kSTs0"])
                S.op("dve", lambda e: e.tensor_copy(kSTs[:, 4:8, :].rearrange("p h n -> p (h n)"), B[3][:]),
                     reads=bq(3), writes=["kSTs1"])
                S.op("dve", lambda e: e.tensor_tensor(gsel[:], cm[:, 10, 0:16].unsqueeze(1).to_broadcast([128, 4, 16]),
                                                      sc(AD).unsqueeze(2).to_broadcast([128, 4, 16]), ALU.mult),
                     reads=[N("ad"), "cm"], writes=["gsel"])
                for h in range(4):
                    S.op("pe", lambda e: e.matmul(B[2][:, h * 16:(h + 1) * 16], onesf[:], gsel[:, h, :],
                                                  start=True, stop=True), reads=["gsel", "onesf"], writes=bq(2, 0))
                S.op("act", lambda e: e.activation(eglb[:].rearrange("p h u -> p (h u)"), B[2][:, 0:64], AF.Exp),
                     reads=bq(2, 0), writes=["eglb"])
            HS = [slice(h * 128, (h + 1) * 128) for h in range(4)]
            for h in range(4):
                Bh = B[4 + h]
                if not smp:
                    S.op("pe", lambda e: e.matmul(Bh[:, KS], cT[:, 4 + h, :], Sbf[:, h, :], start=True, stop=True),
                         reads=[N("cT"), ("Sbf", h)], writes=bq(4 + h, 0))
                    if own:
                        S.op("pe", lambda e: e.matmul(Bh[:, QS], cT[:, h, :], Sbf[:, h, :], start=True, stop=True),
                             reads=[N("cT"), ("Sbf", h)], writes=bq(4 + h, 1))
                else:
                    S.op("pe", lambda e: e.matmul(Bh[:, KS], kSTs[:, h, :], identf, start=True, stop=True),
                         reads=["kSTs0", "cm"], writes=bq(4 + h, 0))
                    S.op("pe", lambda e: e.matmul(Bh[:, QS], kSTs[:, 4 + h, :], identf, start=True, stop=True),
                         reads=["kSTs1", "cm"], writes=bq(4 + h, 1))
            for h in range(4):
                Bh = B[4 + h]
                S.op("dve", lambda e: e.scalar_tensor_tensor(R32[:, h, :], Bh[:, KS], sc(NBG, h), vb[:, h, :],
                                                             ALU.mult, ALU.add),
                     reads=bq(4 + h, 0) + [N("nbG"), N(("vb", h))], writes=[("R32", h)])
                S.op("act", lambda e: e.copy(Rr[:, h, :], R32[:, h, :]), reads=[("R32", h)], writes=[("Rr", h)])
                if own:
                    S.op("act", lambda e: e.activation(tmpo[:, h, :], Bh[:, QS], AF.Copy, scale=sc(EG, h)),
                         reads=bq(4 + h, 1) + [N("eG")], writes=[("tmpo", h)])
            yield
            for h in range(4):
                Bh = B[4 + h]
                S.op("pe", lambda e: e.matmul(Bh[:, VS], Ainv[:, h, :], Rr[:, h, :], start=True, stop=True),
                     reads=[akey, ("Rr", h)], writes=bq(4 + h, 2))
            for h in range(4):
                Bh = B[4 + h]
                S.op("act", lambda e: e.copy(v1f[:, h, :], Bh[:, VS]), reads=bq(4 + h, 2), writes=[("v1f", h)])
                S.op("dve", lambda e: e.tensor_copy(v1b[:, h, :], v1f[:, h, :]), reads=[("v1f", h)], writes=[("v1b", h)])
                S.op("dve", lambda e: e.tensor_tensor(t1[:, h, :], R32[:, h, :], v1f[:, h, :], ALU.subtract),
                     reads=[("v1f", h), ("R32", h)], writes=[("t1", h)])
            yield
            for h in range(4):
                Bh = B[4 + h]
                S.op("pe", lambda e: e.matmul(Bh[:, KS], MT[2][:, h, :], v1b[:, h, :], start=True, stop=True),
                     reads=[N(("MT", 2)), ("v1b", h)], writes=bq(4 + h, 0))
            for h in range(4):
                Bh = B[4 + h]
                S.op("dve", lambda e: e.tensor_tensor(r1b[:, h, :], t1[:, h, :], Bh[:, KS], ALU.add),
                     reads=bq(4 + h, 0) + [("t1", h)], writes=[("r1b", h)])
            yield
            for h in range(4):
                Bh = B[4 + h]
                S.op("pe", lambda e: e.matmul(Bh[:, QS], Ainv[:, h, :], r1b[:, h, :], start=True, stop=True),
                     reads=[akey, ("r1b", h)], writes=bq(4 + h, 1))
            for h in range(4):
                Bh = B[4 + h]
                S.op("dve", lambda e: e.tensor_tensor(vnew[:, h, :], v1f[:, h, :], Bh[:, QS], ALU.add),
                     reads=bq(4 + h, 1) + [("v1f", h)], writes=[("vnew", h)])
            yield
            for h in range(4):
                Bh = B[4 + h]
                if own:
                    S.op("pe", lambda e: e.matmul(Bh[:, VS], qkT[:, h, :], vnew[:, h, :], start=True, stop=True),
                         reads=[N("qkT"), ("vnew", h)], writes=bq(4 + h, 2))
                if not smp:
                    S.op("pe", lambda e: e.matmul(Bh[:, OS], kd[:, h, :], vnew[:, h, :], start=True, stop=True),
                         reads=[N(("kd", h)), ("vnew", h)], writes=bq(4 + h, 3))
            for h in range(4):
                Bh = B[4 + h]
                if own:
                    S.op("dve", lambda e: e.tensor_tensor(o32[:, h, :], tmpo[:, h, :], Bh[:, VS], ALU.add),
                         reads=bq(4 + h, 2) + [("tmpo", h)], writes=[("o32", h)])
                if not smp:
                    S.op("dve", lambda e: e.scalar_tensor_tensor(S32[:, h, :], S32[:, h, :], sc(EGL, h), Bh[:, OS],
                                                                 ALU.mult, ALU.add),
                         reads=bq(4 + h, 3) + [N("egl"), ("S32", h)], writes=[("S32", h)])
                    S.op("act", lambda e: e.copy(Sbf[:, h, :], S32[:, h, :]), reads=[("S32", h)], writes=[("Sbf", h)])
            yield
            if smp:
                for h in range(4):
                    S.op("pool", lambda e: e.tensor_tensor(
                        vblk, vnew[:, h, :].unsqueeze(1).to_broadcast([128, 16, 128]),
                        cm[:, 10, 0:16].unsqueeze(2).to_broadcast([128, 16, 128]), ALU.mult),
                        reads=[("vnew", h), "cm"], writes=ACCK)
                    for q4 in range(4):
                        bi = (0, 1, 2, 3)[q4]
                        S.op("pe", lambda e: e.matmul(B[bi][:], kd[:, h, :],
                                                      vblk[:, 4 * q4:4 * q4 + 4, :].rearrange("p u v -> p (u v)"),
                                                      start=True, stop=True),
                             reads=[N(("kd", h))] + ACCK, writes=bq(bi))
                        sl = Ssm32b[h % 2][:, 4 * q4:4 * q4 + 4, :]
                        S.op("pool", lambda e: e.tensor_tensor(
                            sl, sl, eglb[:, h, 4 * q4:4 * q4 + 4].unsqueeze(2).to_broadcast([128, 4, 128]), ALU.mult),
                            reads=["eglb", ("Ssm32", h % 2)], writes=[("Ssm32", h % 2)])
                        S.op("dve", lambda e: e.tensor_tensor(sl, sl, B[bi][:].rearrange("p (u v) -> p u v", u=4), ALU.add),
                             reads=bq(bi) + [("Ssm32", h % 2)], writes=[("Ssm32", h % 2)])
                    S.dma("sp", io["ds_o"][:, h].rearrange("u k v -> k u v"), Ssm32b[h % 2][:], reads=[("Ssm32", h % 2)])
                    if h + 2 < 4:
                        S.dma("sp", Ssm32b[h % 2][:], io["sdelta"][:, h + 2].rearrange("u k v -> k u v"),
                              writes=[("Ssm32", h % 2)])
            if own:
                okeys = [("o32", h) for h in range(4)]
                S.op("act", lambda e: e.activation(sqo, o32[:].rearrange("p h n -> p (h n)"), AF.Square),
                     reads=okeys, writes=T1K)
                S.op("dve", lambda e: e.tensor_reduce(sc(SSO), sqo.rearrange("p (h d) -> p h d", d=128), AX.X, ALU.add),
                     reads=T1K, writes=["sso"])
                S.op("act", lambda e: e.activation(sc(RSO), sc(SSO), AF.Sqrt, bias=EPS, scale=1.0 / 128),
                     reads=["sso"], writes=["rso"])
                S.op("dve", lambda e: e.reciprocal(sc(RSO), sc(RSO)), reads=["rso"], writes=["rso"])
                S.op("dve", lambda e: e.tensor_tensor(o32[:], o32[:], sc(RSO).unsqueeze(2).to_broadcast([128, 4, 128]), ALU.mult),
                     reads=okeys + ["rso"], writes=okeys)
                S.op("pool", lambda e: e.tensor_tensor(o32[:], o32[:], dng[:].unsqueeze(1).to_broadcast([128, 4, 128]), ALU.mult),
                     reads=okeys + ["dng"], writes=okeys)
                S.op("pool", lambda e: e.tensor_tensor(obf[:], o32[:].rearrange("p h n -> p (h n)"), zs[:], ALU.mult),
                     reads=okeys + [N("zs")], writes=["obf"])
                for h in range(4):
                    S.op("pe", lambda e: e.matmul(B[3][:, h * 128:(h + 1) * 128], obf[:, h * 128:(h + 1) * 128], identb[:],
                                                  start=True, stop=True), reads=["obf", "ident"], writes=bq(3, h))
                S.op("act", lambda e: e.copy(self.mixT[:, 0:4, orow:orow + 128], B[3][:].rearrange("p (h n) -> p h n", h=4)),
                     reads=bq(3), writes=[("mixT", t)])
            if t == 31:
                S.dma("sp", io["dp_o"].rearrange("h k v -> k h v"), S32[:], reads=[("S32", h) for h in range(4)])
                for ch in range(12):
                    bi = 2 + ch // 4
                    S.op("pe", lambda e: e.matmul(B[bi][0:3, (ch % 4) * 128:(ch % 4 + 1) * 128], ub[:, ch, 128:131],
                                                  identf, start=True, stop=True), reads=[("uT", b), "cm"], writes=bq(bi, ch % 4))
                for gi in range(3):
                    S.op("act", lambda e: e.copy(cps[0:3, gi * 512:(gi + 1) * 512], B[2 + gi][0:3, :]),
                         reads=bq(2 + gi), writes=ACCK)
                S.dma("sp", io["cp_o"], cps[0:3, :], reads=ACCK)
            if smp:
                S.op("dve", lambda e: e.tensor_copy(cs3[:].rearrange("p c (u r) -> p c u r", u=16), uS[:, :, :, 8:11]),
                     reads=["uS"], writes=["cs3"])
                for ch in range(12):
                    bi = 2 + ch // 4
                    S.op("pe", lambda e: e.matmul(B[bi][0:48, (ch % 4) * 128:(ch % 4 + 1) * 128], cs3[:, ch, :],
                                                  identf, start=True, stop=True), reads=["cs3", "cm"], writes=bq(bi, ch % 4))
                for gi in range(3):
                    S.op("act", lambda e: e.copy(cps[:, gi * 512:(gi + 1) * 512], B[2 + gi][0:48, :]),
                         reads=bq(2 + gi), writes=ACCK)
                S.dma("sp", io["cs_o"].rearrange("u r c -> (u r) c"), cps, reads=ACCK)

        alloc_par(1)
        Dg = TP("Dg", [128, 48, 128], BF16)
        ubf = TP("ubf", [128, 12, 131], BF16)
        S.op("dve", lambda e: e.tensor_tensor(Dg[:], identb[:].unsqueeze(1).to_broadcast([128, 48, 128]),
                                              cw[:].rearrange("p c i -> p (c i)").unsqueeze(2).to_broadcast([128, 48, 128]),
                                              ALU.mult), reads=["ident", "cw"], writes=["Dg"])
        self.load_x(0, 0)
        self.load_x(1, 1)
        self.norm_T(0, self.g1, "g1", lnexp=True)
        def drive(gen_scan, gen_pre):
            pre_done = gen_pre is None
            scan_done = gen_scan is None
            while not (pre_done and scan_done):
                if not scan_done:
                    try:
                        next(gen_scan)
                    except StopIteration:
                        scan_done = True
                if not pre_done:
                    for _ in range(3):
                        if next(gen_pre) == "SCAN":
                            pre_done = True
                            break

        gens = [tileA(t) for t in range(32)]
        drive(None, gens[0])
        for t in range(32):
            drive(gens[t], gens[t + 1] if t + 1 < 32 else None)
        S.barrier()
        stP.close()
        uS = T2("uS", [128, 12, 16, 11], F32)
        cs3 = T2("cs3", [128, 12, 48], F32)
        Ssm32b = [T2("Ssm32_%d" % i, [128, 16, 128], F32) for i in range(2)]
        Ssmbb = [T2("Ssmb_%d" % i, [128, 16, 128], BF16) for i in range(2)]
        kSTs = T2("kSTs", [128, 8, 128], F32)
        for _ in tileA(32):
            pass


def run_interleaved(gen_fn, items, depth=2):
    active = []
    nxt = 0
    while nxt < len(items) or active:
        while nxt < len(items) and len(active) < depth:
            active.append(gen_fn(items[nxt]))
            nxt += 1
            break
        for g in list(active):
            try:
                next(g)
            except StopIteration:
                active.remove(g)


_PROG_CACHE = {}


def get_prog(phases):
    key = tuple(phases)
    if key not in _PROG_CACHE:
        p = Prog(phases)
        p.build()
        _PROG_CACHE[key] = p
    return _PROG_CACHE[key]


def make_core_inputs(c, I):
    s, half = c // 2, c % 2
    f = np.float32
    xp = np.asarray(I["x_prompt"], f)
    xs = np.asarray(I["x_sample"], f)
    xl = np.zeros((NLOC, D), f)
    if half == 1:
        xl[0:NPRE] = xp[s, 0:2048]
    xl[NPRE:NPRE + NOWN] = xp[s, half * 2048:(half + 1) * 2048]
    xl[NPRE + NOWN:] = xs[16 * c:16 * c + 16].reshape(128, D)
    w_in = np.asarray(I["w_in"], f)[0]
    m = {
        "xl": xl,
        "w_inA": np.ascontiguousarray(w_in[:, 0:2056]),
        "w_inB": np.ascontiguousarray(w_in[:, 2056:3592]),
        "g1bc": np.ascontiguousarray(np.broadcast_to(np.asarray(I["norm1_g"], f)[0][None, :], (128, D))),
        "qgbc": np.ascontiguousarray(np.broadcast_to(np.asarray(I["q_norm_g"], f)[0][None, :], (128, 64))),
        "kgbc": np.ascontiguousarray(np.broadcast_to(np.asarray(I["k_norm_g"], f)[0][None, :], (128, 64))),
    }
    cw = np.asarray(I["conv_w"], f)[0]
    m["cwT"] = np.ascontiguousarray(cw.T.reshape(12, 128, 4).transpose(1, 0, 2))
    m["alogbc"] = np.ascontiguousarray(np.broadcast_to(np.asarray(I["a_log"], f)[0][None, :], (128, 4)))
    m["dtbbc"] = np.ascontiguousarray(np.broadcast_to(np.asarray(I["dt_bias"], f)[0][None, :], (128, 4)))
    m["dngbc"] = np.ascontiguousarray(np.broadcast_to(np.asarray(I["delta_norm_g"], f)[0][None, :], (128, 128)))
    m["sdelta"] = np.ascontiguousarray(np.asarray(I["state_delta"], f)[0, 16 * c:16 * c + 16])
    m["sconv"] = np.ascontiguousarray(np.asarray(I["state_conv"], f)[0, 16 * c:16 * c + 16])
    m["pvalid"] = np.full((128, 64), float(half), f)
    m["w_o"] = np.asarray(I["w_o"], f)[0]
    m["w_up"] = np.asarray(I["w_up"], f)[0]
    m["w_down"] = np.asarray(I["w_down"], f)[0]
    m["g2bc"] = np.ascontiguousarray(np.broadcast_to(np.asarray(I["norm2_g"], f)[0][None, :], (128, D)))
    m["ck"] = np.asarray(I["cache_swa_k"], f)[0, 16 * c:16 * c + 16].reshape(16, 2048, 512)
    m["cv"] = np.asarray(I["cache_swa_v"], f)[0, 16 * c:16 * c + 16].reshape(16, 2048, 512)
    m.update(host_consts())
    return m


def kernel(_phases=("B", "ATT", "A", "MLP"), **I):
    prog = get_prog(_phases)
    in_maps = []
    for c in range(NCORES):
        m = make_core_inputs(c, I)
        in_maps.append({k: m[k] for k in prog.ins})
    res = run_bass_kernel_spmd(prog.nc, in_maps, core_ids=list(range(NCORES)))
    R = res.results
    f = np.float32
    y_p = np.zeros((4, 4096, D), f)
    y_s = np.zeros((128, 8, D), f)
    k_p = np.zeros((1, 4, 2048, 8, 64), f)
    v_p = np.zeros((1, 4, 2048, 8, 64), f)
    d_p = np.zeros((1, 4, 4, 128, 128), f)
    c_p = np.zeros((1, 4, 3, 1536), f)
    k_s = np.zeros((1, 128, 8, 8, 64), f)
    v_s = np.zeros((1, 128, 8, 8, 64), f)
    d_s = np.zeros((1, 128, 4, 128, 128), f)
    c_s = np.zeros((1, 128, 3, 1536), f)
    for c in range(NCORES):
        s, half = c // 2, c % 2
        r = R[c]
        y_p[s, half * 2048:(half + 1) * 2048] = r["y_o"][0:NOWN]
        y_s[16 * c:16 * c + 16] = r["y_o"][NOWN:].reshape(16, 8, D)
        if half == 1:
            k_p[0, s] = r["k_o"][0:NOWN].reshape(2048, 8, 64)
            v_p[0, s] = r["v_o"][0:NOWN].reshape(2048, 8, 64)
            if "dp_o" in r:
                d_p[0, s] = r["dp_o"]
                c_p[0, s] = r["cp_o"]
        k_s[0, 16 * c:16 * c + 16] = r["k_o"][NOWN:].reshape(16, 8, 8, 64)
        v_s[0, 16 * c:16 * c + 16] = r["v_o"][NOWN:].reshape(16, 8, 8, 64)
        if "ds_o" in r:
            d_s[0, 16 * c:16 * c + 16] = r["ds_o"]
            c_s[0, 16 * c:16 * c + 16] = r["cs_o"]
    return (y_p, y_s, k_p, v_p, d_p, c_p, k_s, v_s, d_s, c_s)
```
